# Optimizing a Trainium2 kernel written in Bass

```python
import jax, jax.numpy as jnp
from jax import lax
import numpy as np

D_MODEL = 4096
BATCH = 2
SEQ = 4096
DEPTH = 2

GRID_W = 64
CTX_LEN = 256
D_POOL = D_MODEL // 2
POOL_WINDOWS = (2, 4, 8, 16)
N_POOL_GROUPS = len(POOL_WINDOWS)
D_POOL_GROUP = D_POOL // N_POOL_GROUPS
HEAD_DIM = 128
D_ATTN = D_MODEL // 2
N_HEADS = D_ATTN // HEAD_DIM
NA_KH_MAX = 8
NA_KW = 16
NA_QB_W = 16
NA_NCB = GRID_W // NA_QB_W
NA_CW = 2 * NA_KW
D_IN = 2 * D_POOL + 4 * D_ATTN + 2 * D_MODEL
RMS_EPS = 1e-6
NEG_INF = -1e30

kernel_name = "hybrid_pool_natten_prefix_dit"


def _rmsnorm(x, g):
    xf = x.astype(jnp.float32)
    y = xf * lax.rsqrt(jnp.mean(xf * xf, axis=-1, keepdims=True) + RMS_EPS)
    return (y * g.astype(jnp.float32)).astype(x.dtype)


def _modulate(x, g, shift, scale):
    return _rmsnorm(x, g) * (1 + scale) + shift


def _split_proj(p):
    sizes = (D_POOL, D_POOL, D_ATTN, D_ATTN, D_ATTN, D_ATTN, D_MODEL, D_MODEL)
    return jnp.split(p, [int(i) for i in np.cumsum(sizes)[:-1]], axis=-1)


def _heads(a):
    b, l, _ = a.shape
    return a.reshape(b, l, N_HEADS, HEAD_DIM)


def _pool_mix(u, w_pool, s_pool):
    b, l, _ = u.shape
    uf = u.astype(jnp.float32)
    csum = jnp.concatenate([jnp.zeros((b, 1, D_POOL), jnp.float32), jnp.cumsum(uf, axis=1)], axis=1)
    t = jnp.arange(l)
    outs = []
    for gi, w in enumerate(POOL_WINDOWS):
        lo = jnp.clip(t - w // 2, 0, l - 1)
        hi = jnp.clip(t - w // 2 + w - 1, 0, l - 1)
        cols = slice(gi * D_POOL_GROUP, (gi + 1) * D_POOL_GROUP)
        window_sum = csum[:, hi + 1, cols] - csum[:, lo, cols]
        count = (hi - lo + 1).astype(jnp.float32)[None, :, None]
        outs.append(window_sum / count - uf[:, :, cols])
    p = jnp.stack(outs, axis=2).astype(u.dtype)
    p = jnp.einsum('blgc,gcd->blgd', p, w_pool).reshape(b, l, D_POOL)
    return p * s_pool


def _na_column_tables():
    j = np.arange(NA_NCB)
    col_start = np.clip(j * NA_QB_W - NA_KW // 2, 0, GRID_W - NA_CW)
    key_col = col_start[:, None] + np.arange(NA_CW)[None, :]
    q_col = j[:, None] * NA_QB_W + np.arange(NA_QB_W)[None, :]
    win_start = np.clip(q_col - NA_KW // 2, 0, GRID_W - NA_KW)
    kc = key_col[:, None, :]
    valid = (kc >= win_start[:, :, None]) & (kc < win_start[:, :, None] + NA_KW)
    dc_idx = np.clip(kc - q_col[:, :, None] + NA_KW - 1, 0, 2 * NA_KW - 2)
    return key_col, valid, dc_idx


def _neighbourhood_attention(q, k, v, kc, vc, rpb):
    b, l = q.shape[:2]
    rows = l // GRID_W
    kh = min(NA_KH_MAX, rows)

    def grid(a):
        return a.transpose(0, 2, 1, 3).reshape(b, N_HEADS, rows, GRID_W, HEAD_DIM)

    qg, kg, vg = grid(q * HEAD_DIM ** -0.5), grid(k), grid(v)
    kct, vct = kc.transpose(0, 2, 1, 3), vc.transpose(0, 2, 1, 3)
    key_col, valid, dc_idx = _na_column_tables()
    n_loc = kh * NA_CW

    def row(r):
        rs = jnp.clip(r - kh // 2, 0, rows - kh)
        q_r = lax.dynamic_index_in_dim(qg, r, axis=2, keepdims=False)
        q_r = q_r.reshape(b, N_HEADS, NA_NCB, NA_QB_W, HEAD_DIM)
        k_rows = lax.dynamic_slice_in_dim(kg, rs, kh, axis=2)
        v_rows = lax.dynamic_slice_in_dim(vg, rs, kh, axis=2)
        k_blk = k_rows[:, :, :, key_col]
        v_blk = v_rows[:, :, :, key_col]
        dr_idx = rs + jnp.arange(kh) - r + NA_KH_MAX - 1
        bias = rpb[:, dr_idx[None, None, :, None], dc_idx[:, :, None, :]]
        s_loc = jnp.einsum('bhjqd,bhrjkd->bhjqrk', q_r, k_blk).astype(jnp.float32)
        s_loc = jnp.where(valid[:, :, None, :], s_loc + bias.astype(jnp.float32)[None], NEG_INF)
        s_ctx = jnp.einsum('bhjqd,bhkd->bhjqk', q_r, kct).astype(jnp.float32)
        s = jnp.concatenate([s_loc.reshape(b, N_HEADS, NA_NCB, NA_QB_W, n_loc), s_ctx], axis=-1)
        p = jax.nn.softmax(s, axis=-1).astype(v.dtype)
        p_loc = p[..., :n_loc].reshape(b, N_HEADS, NA_NCB, NA_QB_W, kh, NA_CW)
        p_ctx = p[..., n_loc:]
        o = (jnp.einsum('bhjqrk,bhrjkd->bhjqd', p_loc, v_blk)
             + jnp.einsum('bhjqk,bhkd->bhjqd', p_ctx, vct))
        return o.reshape(b, N_HEADS, GRID_W, HEAD_DIM)

    out = lax.map(row, jnp.arange(rows))
    return out.transpose(1, 0, 3, 2, 4).reshape(b, l, D_ATTN)


def _context_attention(qc, kc, vc):
    b, lc = qc.shape[:2]
    s = jnp.einsum('bqhd,bkhd->bhqk', qc * HEAD_DIM ** -0.5, kc).astype(jnp.float32)
    p = jax.nn.softmax(s, axis=-1).astype(vc.dtype)
    return jnp.einsum('bhqk,bkhd->bqhd', p, vc).reshape(b, lc, D_ATTN)


def _merge(y_pool, z_pool, y_attn, z_attn, g_pool, g_attn, w_br_pool, w_br_attn, w_out):
    br_pool = (y_pool * jax.nn.silu(z_pool)) @ w_br_pool
    br_attn = (y_attn * jax.nn.silu(z_attn)) @ w_br_attn
    return (jax.nn.sigmoid(g_pool) * br_pool + jax.nn.sigmoid(g_attn) * br_attn) @ w_out


def setup_inputs(seed: int = 0) -> dict:
    key = jax.random.key(seed)
    ks = jax.random.split(key, 16)
    f32 = jnp.float32
    d = D_MODEL
    nrm = lambda k, shape, s: jax.random.normal(k, shape, f32) * s
    return {
        "x": nrm(ks[0], (BATCH, SEQ, d), 1.0),
        "c": nrm(ks[1], (BATCH, d), 1.0),
        "ctx": nrm(ks[2], (BATCH, CTX_LEN, d), 1.0),
        "c_ctx": nrm(ks[3], (d,), 1.0),
        "norm_g": 1.0 + nrm(ks[4], (DEPTH, d), 0.05),
        "w_ada": nrm(ks[5], (DEPTH, d, 3 * d), 0.5 * d ** -0.5),
        "b_ada": nrm(ks[6], (DEPTH, 3 * d), 0.02),
        "w_in": nrm(ks[7], (DEPTH, d, D_IN), d ** -0.5),
        "b_in": nrm(ks[8], (DEPTH, D_IN), 0.02),
        "w_pool": nrm(ks[9], (DEPTH, N_POOL_GROUPS, D_POOL_GROUP, D_POOL_GROUP), D_POOL_GROUP ** -0.5),
        "s_pool": 1.0 + nrm(ks[10], (DEPTH, D_POOL), 0.1),
        "rpb": nrm(ks[11], (DEPTH, N_HEADS, 2 * NA_KH_MAX - 1, 2 * NA_KW - 1), 0.1),
        "w_br_pool": nrm(ks[12], (DEPTH, D_POOL, d), D_POOL ** -0.5),
        "w_br_attn": nrm(ks[13], (DEPTH, D_ATTN, d), D_ATTN ** -0.5),
        "w_out": nrm(ks[14], (DEPTH, d, d), d ** -0.5),
        "final_g": 1.0 + nrm(ks[15], (d,), 0.05),
    }


def reference(x, c, ctx, c_ctx, norm_g, w_ada, b_ada, w_in, b_in, w_pool, s_pool, rpb,
              w_br_pool, w_br_attn, w_out, final_g):
    x_lat, x_ctx = x, ctx
    kv_lo = 2 * D_POOL + D_ATTN
    for l in range(DEPTH):
        last = l == DEPTH - 1
        ada_lat = jax.nn.silu(c) @ w_ada[l] + b_ada[l]
        sh, sc, gt = jnp.split(ada_lat[:, None, :], 3, axis=-1)
        ada_ctx = jax.nn.silu(c_ctx) @ w_ada[l] + b_ada[l]
        sh_c, sc_c, gt_c = jnp.split(ada_ctx, 3)
        h_lat = _modulate(x_lat, norm_g[l], sh, sc)
        h_ctx = _modulate(x_ctx, norm_g[l], sh_c, sc_c)

        u_l, zp_l, q_l, k_l, v_l, za_l, gp_l, ga_l = _split_proj(h_lat @ w_in[l] + b_in[l])
        if last:
            kv_c = h_ctx @ w_in[l][:, kv_lo:kv_lo + 2 * D_ATTN] + b_in[l][kv_lo:kv_lo + 2 * D_ATTN]
            k_c, v_c = jnp.split(kv_c, 2, axis=-1)
        else:
            u_c, zp_c, q_c, k_c, v_c, za_c, gp_c, ga_c = _split_proj(h_ctx @ w_in[l] + b_in[l])
        kc, vc = _heads(k_c), _heads(v_c)

        y_pool_l = _pool_mix(u_l, w_pool[l], s_pool[l])
        y_attn_l = _neighbourhood_attention(_heads(q_l), _heads(k_l), _heads(v_l), kc, vc, rpb[l])
        mix_lat = _merge(y_pool_l, zp_l, y_attn_l, za_l, gp_l, ga_l, w_br_pool[l], w_br_attn[l], w_out[l])

        if not last:
            y_pool_c = _pool_mix(u_c, w_pool[l], s_pool[l])
            y_attn_c = _context_attention(_heads(q_c), kc, vc)
            mix_ctx = _merge(y_pool_c, zp_c, y_attn_c, za_c, gp_c, ga_c, w_br_pool[l], w_br_attn[l], w_out[l])
            x_ctx = x_ctx + gt_c * mix_ctx
        x_lat = x_lat + gt * mix_lat
    return _rmsnorm(x_lat, final_g)
```

```python
import numpy as np
import contextlib
import concourse.bass as bass
import concourse.mybir as mybir
from concourse.bass_utils import run_bass_kernel_spmd

F32 = mybir.dt.float32
BF16 = mybir.dt.bfloat16
AF = mybir.ActivationFunctionType
ALU = mybir.AluOpType

ENGS = ['pe', 'act', 'dve', 'pool', 'sp']
BLOCKNAME = {'pe': 'tensor', 'act': 'scalar', 'dve': 'vector', 'pool': 'gpsimd', 'sp': 'sync'}
NEG = -1e30
POOL_WINDOWS = (2, 4, 8, 16)
NLAT = 1920
NSLOT = 2176
SPEC_ROWS = (8, 9, 10, 11, 21, 22, 23)
DEBUG = False


class Op:
    __slots__ = ('eng', 'fn', 'deps', 'dma', 'ndep', 'sem', 'val', 'gidx')

    def __init__(self, eng, fn, dma):
        self.eng = eng
        self.fn = fn
        self.dma = dma
        self.deps = []
        self.ndep = 0
        self.sem = None
        self.val = 0


class Prog:
    def __init__(self, nc, stack, ndma=8):
        self.nc = nc
        self.stack = stack
        self.ops = {e: [] for e in ENGS}
        self.lastw = {}
        self.readers = {}
        self.ndma = ndma
        self.nops = 0
        self.all_dma = []

    def add(self, eng, fn, reads=(), writes=(), dma=False, same=True, extra=()):
        op = Op(eng, fn, dma)
        op.gidx = self.nops
        self.nops += 1
        deps = {}
        for r in reads:
            for w in self.lastw.get(r, ()):
                deps[id(w)] = w
        for r in writes:
            for w in self.lastw.get(r, ()):
                deps[id(w)] = w
            for rd in self.readers.get(r, ()):
                deps[id(rd)] = rd
        for d in extra:
            deps[id(d)] = d
        for d in deps.values():
            if d is op:
                continue
            if (not same) and (not d.dma) and d.eng == eng:
                continue
            op.deps.append(d)
            d.ndep += 1
        for r in writes:
            self.lastw[r] = [op]
            self.readers[r] = []
        for r in reads:
            lst = self.readers.setdefault(r, [])
            if not dma:
                for i, o in enumerate(lst):
                    if (not o.dma) and o.eng == eng:
                        lst[i] = op
                        break
                else:
                    lst.append(op)
            else:
                lst.append(op)
        self.ops[eng].append(op)
        if dma:
            self.all_dma.append(op)
        return op

    def barrier(self, engines=('pe', 'act', 'dve', 'sp'), dma_queues=('sp', 'act')):
        deps = []
        for e in ('pe', 'act', 'dve'):
            for o in reversed(self.ops[e]):
                if o.fn is not None:
                    deps.append(o)
                    break
        for q in dma_queues:
            dl = [o for o in self.ops[q] if o.dma]
            deps.extend(dl[-self.ndma:])
        for e in engines:
            self.add(e, None, extra=[d for d in deps])

    def emit(self):
        nc = self.nc
        engsem = {}
        for e in ENGS:
            engsem[e] = self.stack.enter_context(nc.semaphore(f"s_{e}"))
        dmasem = {}
        for e in ENGS:
            nd = sum(1 for o in self.ops[e] if o.dma)
            if nd:
                dmasem[e] = [self.stack.enter_context(nc.semaphore(f"d_{e}{i}")) for i in range(min(self.ndma, nd))]
        self.maxcnt = {}
        for e in ENGS:
            cnt = 0
            nd = 0
            K = len(dmasem.get(e, []))
            for op in self.ops[e]:
                if op.fn is None:
                    continue
                if op.dma:
                    op.sem = dmasem[e][nd % K]
                    op.val = 16 * (nd // K + 1)
                    nd += 1
                elif op.ndep > 0:
                    cnt += 1
                    op.sem = engsem[e]
                    op.val = cnt
            self.maxcnt[e] = cnt
        final_dma = {}
        for op in self.all_dma:
            final_dma[id(op.sem)] = (op.sem, max(op.val, final_dma.get(id(op.sem), (None, 0))[1]))
        with nc.Block() as blk:
            for e in ENGS:
                ops = self.ops[e]
                if not ops and e != 'sp':
                    continue

                def body(engine, e=e, ops=ops):
                    waited = {}

                    def wait(sem, val):
                        if waited.get(id(sem), 0) >= val:
                            return
                        engine.wait_ge(sem, val)
                        waited[id(sem)] = val

                    for op in ops:
                        for d in sorted(op.deps, key=lambda o: o.gidx):
                            wait(d.sem, d.val)
                        if op.fn is None:
                            continue
                        if op.dma and op.val > 16:
                            wait(op.sem, op.val - 16)
                        ins = op.fn(engine)
                        if op.dma:
                            ins.then_inc(op.sem, 16)
                        elif op.sem is not None:
                            ins.then_inc(op.sem, 1)
                    if e == 'sp':
                        for sem, val in final_dma.values():
                            wait(sem, val)

                getattr(blk, BLOCKNAME[e])(body)


def split_tiles(a, b, step):
    out = []
    while a < b:
        n = min(step, b - a)
        out.append((a, n))
        a += n
    return out


def build(D, L=2):
    KC = D // 128
    HC = KC // 2
    DP = D // 2
    NH = HC
    GP = D // 8
    GC = GP // 128
    NCB = 5 * KC
    NBUF = 1728
    nc = bass.Bass("TRN2", target_bir_lowering=False)

    def din(name, shape, dt=F32):
        return nc.dram_tensor(name, shape, dt, kind="ExternalInput").ap()

    xw = din("xw", [NLAT, D])
    ctxb = din("ctxb", [256, D])
    cT = din("cT", [128, KC * 2])
    ident_in = din("ident", [128, 128])
    w_in_t = din("w_in_t", [L, NCB, 128, KC * 128])
    w_ada_t = din("w_ada_t", [L, 3 * KC, 128, KC * 128])
    w_out_t = din("w_out_t", [L, KC, 128, KC * 128])
    w_brp_t = din("w_brp_t", [L, KC, 128, HC * 128])
    w_bra_t = din("w_bra_t", [L, KC, 128, HC * 128])
    w_pool_t = din("w_pool_t", [L, 4, 128, GC * GP])
    b_in_T = din("b_in_T", [L, 128, NCB])
    b_ada_T = din("b_ada_T", [L, 128, 3 * KC])
    g_T = din("g_T", [L, 128, KC])
    sp_T = din("sp_T", [L, 128, HC])
    fg_rep = din("fg_rep", [128, D])
    treg = din("treg", [L, NH, 128, 9 * 64])
    tspec = din("tspec", [L, NH, 128, 7 * 6 * 64])
    vm_in = din("vm", [128, NLAT + 16])
    invl_in = din("invl", [128, 64])
    invc_in = din("invc", [128, 64])
    y = nc.dram_tensor("y", [1024, D], F32, kind="ExternalOutput").ap()

    def dscr(name, shape, dt):
        return nc.dram_tensor(name, shape, dt, kind=("ExternalOutput" if DEBUG else "Internal")).ap()

    PT = [dscr(f"PT{l}", [5 * D, NSLOT], BF16) for l in range(L)]
    AOT = [dscr(f"AOT{l}", [DP, NSLOT], BF16) for l in range(L)]
    x1 = dscr("x1", [NLAT, D], F32)
    c1 = dscr("c1", [256, D], F32)
    x2 = dscr("x2", [NLAT, D], F32)
    MTD = [dscr(f"MTD{l}", [D, NSLOT], BF16) for l in range(L)]

    base = [nc.sbuf_base]

    def sb_at(name, shape, dt, off):
        return nc.alloc_sbuf_tensor_at(name, shape, dt, offset=off)

    cur = [((nc.sbuf_base + 63) // 64) * 64]

    def sb(name, shape, dt):
        esz = 4 if dt == F32 else 2
        n = 1
        for s in shape[1:]:
            n *= s
        nbytes = ((n * esz + 63) // 64) * 64
        t = sb_at(name, shape, dt, cur[0])
        cur[0] += nbytes
        return t

    identf = sb("identf", [128, 128], F32)
    identb = sb("identb", [128, 128], BF16)
    onesb = sb("onesb", [128, 128], BF16)
    cTs = sb("cTs", [128, KC * 2], F32)
    scin = sb("scin", [128, KC, 2], BF16)
    bT = sb("bT", [128, NCB], F32)
    bqs = sb("bqs", [128, HC], F32)
    badaTL = [sb(f"badaT{i}", [128, 3 * KC], F32) for i in range(2)]
    gTsL = [sb(f"gTs{i}", [128, KC], F32) for i in range(2)]
    spTs = sb("spTs", [128, HC], F32)
    adaL = [sb(f"ada{i}", [128, 3 * KC, 2], F32) for i in range(2)]
    AtabL = [sb(f"Atab{i}", [128, 2, KC], F32) for i in range(2)]
    vm = sb("vm", [128, NLAT + 16], F32)
    invl = sb("invl", [128, 64], F32)
    invc = sb("invc", [128, 64], F32)
    ss = sb("ss", [128, 8], F32)
    zb = sb("zb", [128, 2 * HC, 64], BF16)
    stg = [sb(f"stg{i}", [128, 512], BF16) for i in range(4)]
    NW = 3
    wsl = [sb(f"wsl{i}", [128, KC * 128], BF16) for i in range(NW)]
    a1 = cur[0]
    hT = sb("hT", [128, KC, NBUF], BF16)
    a1_end = cur[0]
    a2 = cur[0]
    xt = [sb(f"xt{i}", [128, D], F32) for i in range(2)]
    xhb = [sb(f"xh{i}", [128, D], BF16) for i in range(2)]
    a2_end = cur[0]
    assert cur[0] <= nc.SBUF_PARTITION_SIZE_BYTES - 64, cur[0]

    cur[0] = a1
    KTh = [sb(f"KTh{i}", [128, NSLOT], BF16) for i in range(2)]
    VTh = [sb(f"VTh{i}", [128, NSLOT], BF16) for i in range(2)]
    QTh = [sb(f"QTh{i}", [128, NSLOT], BF16) for i in range(2)]
    Vh = [sb(f"Vh{i}", [128, 17, 128], BF16) for i in range(2)]
    tregs = [sb(f"tregs{i}", [128, 9, 64], F32) for i in range(2)]
    tspecs = [sb(f"tspecs{i}", [128, 7, 6, 64], F32) for i in range(2)]
    tregb = [sb(f"tregb{i}", [128, 9, 64], BF16) for i in range(2)]
    tspecb = [sb(f"tspecb{i}", [128, 7, 6, 64], BF16) for i in range(2)]
    sbf = [sb(f"sbf{i}", [128, 512], F32) for i in range(2)]
    pTb = [sb(f"pTb{i}", [128, 512], BF16) for i in range(4)]
    rsb = [sb(f"rsb{i}", [128, 512], F32) for i in range(2)]
    ostg = [sb(f"ostg{i}", [128, 512], BF16) for i in range(2)]
    assert cur[0] <= a1_end
    cur[0] = a1
    MT = sb("MT", [128, KC, 1024], BF16)
    cur[0] = a1
    YP = sb("YP", [128, HC, 1024], BF16)
    YA = sb("YA", [128, HC, 1024], BF16)
    mts = [sb(f"mts{i}", [128, 512], BF16) for i in range(2)]
    wpool = sb("wpool", [128, 4, GC, GP], BF16)
    ub = [sb(f"ub{i}", [128, 528], BF16) for i in range(2)]
    um = [sb(f"um{i}", [128, 528], F32) for i in range(2)]
    pa = sb("pa", [128, 528], F32)
    pb = sb("pb", [128, 528], F32)
    pTg = [sb(f"pTg{i}", [128, GC, 512], BF16) for i in range(2)]
    zt = [sb(f"zt{i}", [128, 512], BF16) for i in range(3)]
    at_ = [sb(f"at{i}", [128, 512], BF16) for i in range(3)]
    assert cur[0] <= a1_end, (cur[0], a1_end)
    cur[0] = a1
    fgs = sb("fgs", [128, D], F32)
    cur[0] = a2
    tT = [[sb(f"tT{j}_{i}", [128, 512], F32) for i in range(4)] for j in range(2)]
    xr = [sb(f"xr{i}", [128, 512], F32) for i in range(3)]
    xn = [sb(f"xn{i}", [128, 512], F32) for i in range(3)]
    t1 = [sb(f"t1{i}", [128, 512], F32) for i in range(2)]
    t2 = [sb(f"t2{i}", [128, 512], F32) for i in range(2)]
    gpt = [sb(f"gpt{i}", [128, 512], BF16) for i in range(2)]
    gat = [sb(f"gat{i}", [128, 512], BF16) for i in range(2)]
    assert cur[0] <= a2_end, (cur[0], a2_end)

    psf = [nc.alloc_psum_tensor(f"psf{i}", [128, 512], F32) for i in range(8)]
    psb6 = psf[6].ap().bitcast(BF16)
    psb7 = psf[7].ap().bitcast(BF16)
    psb = [psb6, psb7]

    with contextlib.ExitStack() as st:
        P = Prog(nc, st, ndma=8)
        cnt = {'ps': 0, 'w': 0, 'stg': 0, 'x': 0, 'bt': 0}

        def dma(q, out, in_, reads, writes, **kw):
            return P.add(q, lambda e: e.dma_start(out=out, in_=in_, **kw), reads=reads, writes=writes, dma=True)

        def wload(src_ap, ncols):
            half = ncols <= HC * 128
            hptr = cnt['w'] % (2 * NW)
            if (not half) and (hptr % 2 == 1):
                cnt['w'] += 1
                hptr = cnt['w'] % (2 * NW)
            i, j = hptr // 2, hptr % 2
            if half:
                cnt['w'] += 1
                dst = wsl[i][:, j * HC * 128:j * HC * 128 + ncols]
                res = [('wh', i, j)]
            else:
                cnt['w'] += 2
                dst = wsl[i][:, 0:ncols]
                res = [('wh', i, 0), ('wh', i, 1)]
            P.add('pool', lambda e: e.dma_start(out=dst, in_=src_ap, max_dma_last_dim=4096), reads=(), writes=res, dma=True)
            return dst, res

        def nextps():
            i = cnt['ps'] % 4
            cnt['ps'] += 1
            return psf[i], ('ps', i)

        def mm(out, lhsT, rhs, start, stop, reads, writes, skip=False):
            if skip:
                P.add('pe', lambda e: e.matmul(out, lhsT, rhs, start=start, stop=stop, skip_group_check=True),
                      reads=reads, writes=writes, same=False)
            else:
                P.add('pe', lambda e: e.matmul(out, lhsT, rhs, start=start, stop=stop),
                      reads=reads, writes=writes, same=False)

        def act(out, in_, func, reads, writes, bias=None, scale=None, accum_out=None):
            kw = {}
            if bias is not None:
                kw['bias'] = bias
            if scale is not None:
                kw['scale'] = scale
            if accum_out is not None:
                kw['accum_out'] = accum_out
            P.add('act', lambda e: e.activation(out=out, in_=in_, func=func, **kw), reads=reads, writes=writes)

        def tt(out, in0, in1, op, reads, writes, eng='dve'):
            P.add(eng, lambda e: e.tensor_tensor(out=out, in0=in0, in1=in1, op=op), reads=reads, writes=writes)

        def tsc(out, in0, s1, s2, op0, op1, reads, writes):
            if op1 is None:
                P.add('dve', lambda e: e.tensor_scalar(out=out, in0=in0, scalar1=s1, scalar2=None, op0=op0),
                      reads=reads, writes=writes)
            else:
                P.add('dve', lambda e: e.tensor_scalar(out=out, in0=in0, scalar1=s1, scalar2=s2, op0=op0, op1=op1),
                      reads=reads, writes=writes)

        def stt(out, in0, scalar, in1, op0, op1, reads, writes):
            P.add('dve', lambda e: e.scalar_tensor_tensor(out=out, in0=in0, scalar=scalar, in1=in1, op0=op0, op1=op1),
                  reads=reads, writes=writes)

        def cp(out, in_, reads, writes, eng='dve'):
            P.add(eng, lambda e: e.tensor_copy(out=out, in_=in_), reads=reads, writes=writes)

        dma('sp', identf[:], ident_in, [], ['identf'])
        dma('sp', cTs[:], cT, [], ['cTs'])
        dma('sp', vm[:], vm_in, [], ['vm'])
        dma('sp', invl[:], invl_in, [], ['invl'])
        dma('sp', invc[:], invc_in, [], ['invc'])
        cp(identb[:], identf[:], ['identf'], ['identb'])
        P.add('dve', lambda e: e.memset(onesb[:], 1.0), writes=['onesb'])
        act(scin[:].rearrange("p k v -> p (k v)"), cTs[:], AF.Silu, ['cTs'], ['scin'])
        P.add('dve', lambda e: e.memset(zb[:], 0.0), writes=['zb'])
        for l_ in range(1, L):
            dma('sp', PT[l_][3 * DP:5 * DP, 1728:1792].rearrange("(c p) s -> p c s", p=128), zb[:], ['zb'], [('PTz', l_)])

        for l in range(L):
            dma('sp', bT[:], b_in_T[l], [], ['bT'])
            dma('sp', spTs[:], sp_T[l], [], ['spTs'])
            tsc(bqs[:], bT[:, 2 * HC:3 * HC], float(128 ** -0.5), None, ALU.mult, None, ['bT'], ['bqs'])

            def make_ada_steps(l2):
                ada_, Atab_, badaT_, gTs_ = adaL[l2 % 2], AtabL[l2 % 2], badaTL[l2 % 2], gTsL[l2 % 2]
                psA, rA = psf[4], ('ps', 4)
                lst = []

                def loads():
                    dma('sp', badaT_[:], b_ada_T[l2], [], [('badaT', l2 % 2)])
                    dma('sp', gTs_[:], g_T[l2], [], [('gTs', l2 % 2)])
                lst.append(loads)
                for cb in range(3 * KC):
                    def step(cb=cb):
                        wt, wr = wload(w_ada_t[l2, cb], KC * 128)
                        for kc in range(KC):
                            mm(psA[:, 2 * cb:2 * cb + 2], wt[:, kc * 128:(kc + 1) * 128], scin[:, kc, :], kc == 0,
                               kc == KC - 1, wr + ['scin'], [rA], skip=True)
                    lst.append(step)

                def fin_a():
                    for v in range(2):
                        tt(ada_[:, 0:2 * KC, v], psA[:, 0:4 * KC].rearrange("p (c v) -> p c v", v=2)[:, :, v],
                           badaT_[:, 0:2 * KC], ALU.add, [rA, ('badaT', l2 % 2)], [('ada', l2 % 2, v)])
                        stt(Atab_[:, v, :], ada_[:, KC:2 * KC, v], 1.0, gTs_[:], ALU.add, ALU.mult,
                            [('ada', l2 % 2, v), ('gTs', l2 % 2)], [('A', l2 % 2, v)])

                def fin_b():
                    for v in range(2):
                        tt(ada_[:, 2 * KC:3 * KC, v], psA[:, 4 * KC:6 * KC].rearrange("p (c v) -> p c v", v=2)[:, :, v],
                           badaT_[:, 2 * KC:3 * KC], ALU.add, [rA, ('badaT', l2 % 2)], [('adag', l2 % 2, v)])
                lst.insert(1 + 2 * KC, fin_a)
                lst.append(fin_b)
                return lst

            def ada_pop():
                if pending_ada:
                    pending_ada.pop(0)()

            if l == 0:
                steps0 = make_ada_steps(0)
                for f_ in steps0[:2 + 2 * KC]:
                    f_()
                pending_ada = steps0[2 + 2 * KC:]
                if L > 1:
                    pending_ada = pending_ada + make_ada_steps(1)
            else:
                for f_ in pending_ada:
                    f_()
                pending_ada = []
                if l + 1 < L:
                    pending_ada = make_ada_steps(l + 1)
            ada, Atab = adaL[l % 2], AtabL[l % 2]
            lp = l % 2

            if DEBUG:
                dbg = dscr(f"dbg_ada{l}", [128, 3 * KC * 2], F32)
                dma('sp', dbg, ada[:].rearrange("p c v -> p (c v)"), [('ada', lp, 0), ('ada', lp, 1)], [('dbgada', l)])

            def Bv(v, kc):
                return ada[:, kc, v:v + 1]

            def Gv(v, kc):
                return ada[:, 2 * KC + kc, v:v + 1]

            if l == 0:
                passes = [
                    dict(segs=[(256, 1472, 0), (1920, 256, 1)],
                         fam_tiles=None),
                    dict(segs=[(0, 256, 0), (1728, 192, 0)], fam_tiles='ukv'),
                ]
            else:
                passes = [dict(segs=[(256, 1472, 0), (1920, 256, 1)], fam_tiles='l1')]
            xsrc_lat = xw if l == 0 else x1
            xsrc_ctx = ctxb if l == 0 else c1
            for ps_ in passes:
                P.barrier()
                boff = 0
                seginfo = []
                for (s0, n, isctx) in ps_['segs']:
                    seginfo.append((s0, n, isctx, boff))
                    for (ts, tn) in split_tiles(0, n, 128):
                        xi = cnt['x'] % 2
                        cnt['x'] += 1
                        xh = xhb[xi]
                        sc0 = 4 * xi
                        src = (xsrc_ctx[ts:ts + tn, :] if isctx else xsrc_lat[s0 + ts:s0 + ts + tn, :])
                        dma('sp', xt[xi][0:tn, :], src, [], [('xt', xi)])
                        act(xh[0:tn, :], xt[xi][0:tn, :], AF.Square, [('xt', xi)], [('xh', xi), ('ss0', xi)],
                            accum_out=ss[0:tn, sc0:sc0 + 1])
                        tsc(ss[0:tn, sc0 + 1:sc0 + 2], ss[0:tn, sc0:sc0 + 1], 1.0 / D, 1e-6, ALU.mult, ALU.add, [('ss0', xi)],
                            [('ss1', xi)])
                        act(ss[0:tn, sc0 + 2:sc0 + 3], ss[0:tn, sc0 + 1:sc0 + 2], AF.Sqrt, [('ss1', xi)], [('ss2', xi)])
                        P.add('dve', lambda e, tn=tn, sc0=sc0: e.reciprocal(out=ss[0:tn, sc0 + 3:sc0 + 4],
                                                                          in_=ss[0:tn, sc0 + 2:sc0 + 3]),
                              reads=[('ss2', xi)], writes=[('ss3', xi)])
                        tsc(xh[0:tn, :], xt[xi][0:tn, :], ss[0:tn, sc0 + 3:sc0 + 4], None, ALU.mult, None,
                            [('xt', xi), ('ss3', xi)], [('xh', xi)])
                        bpos = boff + ts
                        for k8 in range(0, KC, 8):
                            pi = cnt['bt'] % 2
                            cnt['bt'] += 1
                            for kk in range(8):
                                kc = k8 + kk
                                P.add('pe', lambda e, kc=kc, kk=kk, pi=pi, tn=tn, xh=xh: e.transpose(
                                    out=psb[pi][:, kk * 128:kk * 128 + tn], in_=xh[0:tn, kc * 128:(kc + 1) * 128],
                                    identity=identb[0:tn, 0:tn]), reads=[('xh', xi), 'identb'], writes=[('ps', 6 + pi)],
                                    same=False)
                            for kk in range(8):
                                kc = k8 + kk
                                if kk % 2 == 0:
                                    act(hT[:, kc, bpos:bpos + tn], psb[pi][:, kk * 128:kk * 128 + tn], AF.Identity,
                                        [('ps', 6 + pi), ('A', lp, isctx), ('ada', lp, isctx)], [('hT', bpos // 128)],
                                        bias=Bv(isctx, kc), scale=Atab[:, isctx, kc:kc + 1])
                                else:
                                    tsc(hT[:, kc, bpos:bpos + tn], psb[pi][:, kk * 128:kk * 128 + tn],
                                        Atab[:, isctx, kc:kc + 1], Bv(isctx, kc), ALU.mult, ALU.add,
                                        [('ps', 6 + pi), ('A', lp, isctx), ('ada', lp, isctx)], [('hT', bpos // 128)])
                    boff += n

                def tiles_for(fam):
                    res = []
                    ft = ps_['fam_tiles']
                    for (s0, n, isctx, bo) in seginfo:
                        lo, hi = 0, n
                        if ft == 'l1':
                            if isctx:
                                if fam not in ('k', 'v'):
                                    continue
                            elif fam == 'u':
                                lo, hi = (7 - 4) * 64, (25 - 4) * 64
                            elif fam not in ('k', 'v'):
                                lo, hi = (8 - 4) * 64, (24 - 4) * 64
                        elif ft == 'ukv':
                            if fam not in ('u', 'k', 'v'):
                                continue
                        for (a, m) in split_tiles(lo, hi, 512):
                            res.append((bo + a, m, s0 + a))
                    return res

                fams = [('u', HC, AF.Identity), ('zp', HC, AF.Silu), ('q', HC, AF.Identity), ('k', HC, AF.Identity),
                        ('v', HC, AF.Identity), ('za', HC, AF.Silu), ('gp', KC, AF.Sigmoid), ('ga', KC, AF.Sigmoid)]
                cb = 0
                for (fam, nblk, func) in fams:
                    tl = tiles_for(fam)
                    for j in range(nblk):
                        if tl:
                            wt, wr = wload(w_in_t[l, cb], KC * 128)
                            for (bs, n, s0) in tl:
                                ps, pr = nextps()
                                hres = [('hT', t) for t in range(bs // 128, (bs + n + 127) // 128)]
                                for kc in range(KC):
                                    mm(ps[:, 0:n], wt[:, kc * 128:(kc + 1) * 128], hT[:, kc, bs:bs + n], kc == 0,
                                       kc == KC - 1, wr + hres, [pr])
                                si = cnt['stg'] % 4
                                cnt['stg'] += 1
                                if fam == 'q':
                                    act(stg[si][:, 0:n], ps[:, 0:n], func, [pr, 'bqs'], [('stg', si)],
                                        bias=bqs[:, j:j + 1], scale=float(128 ** -0.5))
                                else:
                                    act(stg[si][:, 0:n], ps[:, 0:n], func, [pr, 'bT'], [('stg', si)],
                                        bias=bT[:, cb:cb + 1])
                                dma('sp', PT[l][cb * 128:(cb + 1) * 128, s0:s0 + n], stg[si][:, 0:n], [('stg', si)],
                                    [('PT', l, cb, s0)])
                            ada_pop()
                        cb += 1
                assert cb == NCB

            P.barrier()
            qoff = 2 * DP
            koff = 2 * DP + DP
            voff = 2 * DP + 2 * DP
            if l == 0:
                groups = [(4, 8), (12, 8), (20, 7)]
            else:
                groups = [(8, 8), (16, 8)]
            accs = [(psf[4], ('ps', 4), psf[5], ('ps', 5)), (psf[6], ('ps', 6), psf[7], ('ps', 7))]
            steps = []
            ucnt = [0]
            gcnt = [0]

            def head_loads(h):
                hb = h % 2
                dma('sp', KTh[hb][:], PT[l][koff + h * 128:koff + (h + 1) * 128, :], [], [('KTh', hb)])
                dma('sp', VTh[hb][:], PT[l][voff + h * 128:voff + (h + 1) * 128, :], [], [('VTh', hb)])
                dma('sp', QTh[hb][:], PT[l][qoff + h * 128:qoff + (h + 1) * 128, :], [], [('QTh', hb)])
                dma('sp', tregs[hb][:].rearrange("p a b -> p (a b)"), treg[l, h], [], [('treg', hb)])
                dma('sp', tspecs[hb][:].rearrange("p a b c -> p (a b c)"), tspec[l, h], [], [('tspec', hb)])

            def head_pre(h):
                hb = h % 2
                if h + 1 < NH:
                    head_loads(h + 1)
                cp(tregb[hb][:].rearrange("p a b -> p (a b)"), tregs[hb][:].rearrange("p a b -> p (a b)"),
                   [('treg', hb)], [('tregb', hb)])
                cp(tspecb[hb][:].rearrange("p a b c -> p (a b c)"), tspecs[hb][:].rearrange("p a b c -> p (a b c)"),
                   [('tspec', hb)], [('tspecb', hb)])
                for t8 in range(0, 17, 4):
                    nt = min(4, 17 - t8)
                    ps, pr = nextps()
                    psv = ps.ap().bitcast(BF16)
                    for k in range(nt):
                        st_ = t8 + k
                        P.add('pe', lambda e, st_=st_, k=k, psv=psv, hb=hb: e.transpose(
                            out=psv[:, k * 128:(k + 1) * 128], in_=VTh[hb][:, st_ * 128:(st_ + 1) * 128],
                            identity=identb[:]), reads=[('VTh', hb), 'identb'], writes=[pr], same=False)
                    cp(Vh[hb][:, t8:t8 + nt, :].rearrange("p a b -> p (a b)"), psv[:, 0:nt * 128], [pr], [('Vh', hb)])

            def add_unit(h, steplist, post):
                acc = accs[ucnt[0] % 2]
                ucnt[0] += 1
                for i_, sd in enumerate(steplist):
                    sd['h'] = h
                    sd['hb'] = h % 2
                    sd['acc'] = acc
                    sd['first'] = i_ == 0
                    sd['last'] = i_ == len(steplist) - 1
                    sd['post'] = post if sd['last'] else None
                    steps.append(sd)

            for h in range(NH):
                hb = h % 2
                for (r0, nr) in groups:
                    N = nr * 64
                    oi = gcnt[0] % 2
                    gcnt[0] += 1
                    sl = []
                    for c in range(2):
                        sl.append(dict(ks=NLAT + c * 128, vt=15 + c, qc0=r0 * 64, n=N, oc0=0, bias=None))
                    alo = ((r0 - 4) // 2) * 2
                    for a_ in range(alo, r0 + nr + 3, 2):
                        rlo = max(a_ - 3, r0)
                        rhi = min(a_ + 5, r0 + nr - 1)
                        if rlo > rhi:
                            continue
                        i0 = rlo - a_ + 3
                        cntr = rhi - rlo + 1
                        assert 0 <= a_ and a_ + 1 < 30
                        sl.append(dict(ks=a_ * 64, vt=a_ // 2, qc0=rlo * 64, n=cntr * 64, oc0=(rlo - r0) * 64,
                                       bias=('reg', i0, cntr)))
                    srows = [(si_, r) for si_, r in enumerate(SPEC_ROWS) if r0 <= r < r0 + nr]

                    def out_dma(h=h, r0=r0, N=N, oi=oi):
                        dma('sp', AOT[l][h * 128:(h + 1) * 128, r0 * 64:r0 * 64 + N], ostg[oi][:, 0:N], [('ostg', oi)],
                            [('AOT', l, h, r0)])

                    def post_reg(acc, N=N, oi=oi, has_spec=bool(srows), out_dma=out_dma):
                        psO, rO, psS_, rS_ = acc
                        ri = cnt['stg'] % 2
                        cnt['stg'] += 1
                        P.add('dve', lambda e: e.reciprocal(out=rsb[ri][:, 0:N], in_=psS_[:, 0:N]), reads=[rS_],
                              writes=[('rsb', ri)])
                        tt(ostg[oi][:, 0:N], psO[:, 0:N], rsb[ri][:, 0:N], ALU.mult, [rO, ('rsb', ri)], [('ostg', oi)])
                        if not has_spec:
                            out_dma()

                    add_unit(h, sl, post_reg)
                    if srows:
                        sl = []
                        for k_, (si_, r) in enumerate(srows):
                            for c in range(2):
                                sl.append(dict(ks=NLAT + c * 128, vt=15 + c, qc0=r * 64, n=64, oc0=k_ * 64, bias=None))
                            abase = 4 if r <= 11 else 16
                            for ci in range(6):
                                a_ = abase + 2 * ci
                                sl.append(dict(ks=a_ * 64, vt=a_ // 2, qc0=r * 64, n=64, oc0=k_ * 64, bias=('spec', si_, ci)))
                        j0 = (srows[0][1] - r0) * 64
                        NS = len(srows) * 64

                        def post_spec(acc, NS=NS, j0=j0, oi=oi, out_dma=out_dma):
                            psO, rO, psS_, rS_ = acc
                            ri = cnt['stg'] % 2
                            cnt['stg'] += 1
                            P.add('dve', lambda e: e.reciprocal(out=rsb[ri][:, 0:NS], in_=psS_[:, 0:NS]), reads=[rS_],
                                  writes=[('rsb', ri)])
                            tt(ostg[oi][:, j0:j0 + NS], psO[:, 0:NS], rsb[ri][:, 0:NS], ALU.mult, [rO, ('rsb', ri)],
                               [('ostg', oi)])
                            out_dma()

                        add_unit(h, sl, post_spec)
                if l == 0:
                    oi = gcnt[0] % 2
                    gcnt[0] += 1
                    sl = [dict(ks=NLAT + c * 128, vt=15 + c, qc0=NLAT, n=256, oc0=0, bias=None) for c in range(2)]

                    def post_ctx(acc, oi=oi, h=h):
                        psO, rO, psS_, rS_ = acc
                        ri = cnt['stg'] % 2
                        cnt['stg'] += 1
                        P.add('dve', lambda e: e.reciprocal(out=rsb[ri][:, 0:256], in_=psS_[:, 0:256]), reads=[rS_],
                              writes=[('rsb', ri)])
                        tt(ostg[oi][:, 0:256], psO[:, 0:256], rsb[ri][:, 0:256], ALU.mult, [rO, ('rsb', ri)], [('ostg', oi)])
                        dma('sp', AOT[l][h * 128:(h + 1) * 128, NLAT:NLAT + 256], ostg[oi][:, 0:256], [('ostg', oi)],
                            [('AOT', l, h, 'c')])

                    add_unit(h, sl, post_ctx)

            LA = 3

            def front(i):
                sd = steps[i]
                h, hb = sd['h'], sd['hb']
                if i == 0 or steps[i - 1]['h'] != h:
                    head_pre(h)
                n = sd['n']
                ps, pr = nextps()
                pi_ = i % 4
                sd['pi'] = pi_
                if sd['bias'] is not None:
                    if sd['bias'][0] == 'reg':
                        _, i0, cntr = sd['bias']
                        bias_ap = tregb[hb][:, i0:i0 + cntr, :].rearrange("p a b -> p (a b)")
                        bres = ('tregb', hb)
                    else:
                        _, si_, ci = sd['bias']
                        bias_ap = tspecb[hb][:, si_, ci, :]
                        bres = ('tspecb', hb)
                    mm(ps[:, 0:n], KTh[hb][:, sd['ks']:sd['ks'] + 128], QTh[hb][:, sd['qc0']:sd['qc0'] + n], True, False,
                       [('KTh', hb), ('QTh', hb)], [pr])
                    mm(ps[:, 0:n], identb[:], bias_ap, False, True, ['identb', bres], [pr])
                else:
                    mm(ps[:, 0:n], KTh[hb][:, sd['ks']:sd['ks'] + 128], QTh[hb][:, sd['qc0']:sd['qc0'] + n], True, True,
                       [('KTh', hb), ('QTh', hb)], [pr])
                act(pTb[pi_][:, 0:n], ps[:, 0:n], AF.Exp, [pr], [('pTb', pi_)])

            def back(i):
                sd = steps[i]
                hb = sd['hb']
                psO, rO, psS_, rS_ = sd['acc']
                n, oc0, pi_ = sd['n'], sd['oc0'], sd['pi']
                mm(psO[:, oc0:oc0 + n], Vh[hb][:, sd['vt'], :], pTb[pi_][:, 0:n], sd['first'], sd['last'],
                   [('Vh', hb), ('pTb', pi_)], [rO], skip=True)
                mm(psS_[:, oc0:oc0 + n], onesb[:], pTb[pi_][:, 0:n], sd['first'], sd['last'], ['onesb', ('pTb', pi_)], [rS_],
                   skip=True)
                if sd['post'] is not None:
                    sd['post'](sd['acc'])

            head_loads(0)
            ns_ = len(steps)
            for i in range(ns_ + LA):
                if i < ns_:
                    front(i)
                if i - LA >= 0:
                    back(i - LA)

            P.barrier(engines=('pe', 'act', 'dve', 'sp', 'pool'))
            if l == 0:
                supertiles = [[(256, 512, 0), (768, 512, 0)], [(1280, 448, 0), (1920, 256, 1)]]
            else:
                supertiles = [[(512, 512, 0), (1024, 512, 0)]]
            for g in range(4):
                P.add('pool', lambda e, l=l, g=g: e.dma_start(out=wpool[:, g].rearrange("p c d -> p (c d)"),
                                                           in_=w_pool_t[l, g], max_dma_last_dim=4096),
                      writes=[('wpool', g)], dma=True)
            zoff = DP
            zaoff = 2 * DP + 3 * DP
            gpoff = 6 * DP
            gaoff = 6 * DP + D
            for sti, stl in enumerate(supertiles):
                if sti > 0:
                    P.barrier()
                for si, (s0, T, isctx) in enumerate(stl):
                    co = si * 512
                    for g in range(4):
                        w = POOL_WINDOWS[g]
                        pg = g % 2
                        for cc in range(GC):
                            fc = g * GC + cc
                            ui = cnt['x'] % 2
                            cnt['x'] += 1
                            if isctx:
                                P.add('dve', lambda e, ui=ui, T=T: e.memset(ub[ui][:, 0:T + 16], 0.0), writes=[('ub', ui)])
                                dma('sp', ub[ui][:, 8:8 + T], PT[l][fc * 128:(fc + 1) * 128, s0:s0 + T], [], [('ub', ui)])
                                cp(um[ui][:, 0:T + 16], ub[ui][:, 0:T + 16], [('ub', ui)], [('um', ui)])
                            else:
                                dma('sp', ub[ui][:, 0:T + 16], PT[l][fc * 128:(fc + 1) * 128, s0 - 8:s0 + T + 8], [],
                                    [('ub', ui)])
                                tt(um[ui][:, 0:T + 16], ub[ui][:, 0:T + 16], vm[:, s0:s0 + T + 16], ALU.mult,
                                   [('ub', ui), 'vm'], [('um', ui)])
                            srcb, rsrc = um[ui], ('um', ui)
                            ln = T + 16
                            step = 1
                            bufs = [(pa, 'pa'), (pb, 'pb')]
                            bi = 0
                            while step < w:
                                dst, rdst = bufs[bi]
                                bi ^= 1
                                ln2 = ln - step
                                tt(dst[:, 0:ln2], srcb[:, 0:ln2], srcb[:, step:step + ln2], ALU.add, [rsrc], [rdst])
                                srcb, rsrc = dst, rdst
                                ln = ln2
                                step *= 2
                            o0 = 8 - w // 2
                            stt(pTg[pg][:, cc, 0:T], srcb[:, o0:o0 + T], 1.0 / w, um[ui][:, 8:8 + T], ALU.mult,
                                ALU.subtract, [rsrc, ('um', ui)], [('pTg', pg)])
                            tab = invc if isctx else invl
                            tabr = 'invc' if isctx else 'invl'
                            if isctx:
                                fixes = [(0, 0), (248, 8)]
                            else:
                                fixes = [(fs - s0, to) for (fs, to) in ((512, 0), (1528, 8))
                                         if s0 <= fs and fs + 8 <= s0 + T]
                            for (c0, to) in fixes:
                                dst, rdst = bufs[bi]
                                tt(dst[:, 0:8], srcb[:, o0 + c0:o0 + c0 + 8], tab[:, g * 16 + to:g * 16 + to + 8], ALU.mult,
                                   [rsrc, tabr], [rdst])
                                tt(pTg[pg][:, cc, c0:c0 + 8], dst[:, 0:8], um[ui][:, 8 + c0:8 + c0 + 8], ALU.subtract,
                                   [rdst, ('um', ui)], [('pTg', pg)])
                        for db in range(GC):
                            fcd = g * GC + db
                            ps, pr = nextps()
                            for cc in range(GC):
                                mm(ps[:, 0:T], wpool[:, g, cc, db * 128:(db + 1) * 128], pTg[pg][:, cc, 0:T], cc == 0,
                                   cc == GC - 1, [('wpool', g), ('pTg', pg)], [pr])
                            zi = cnt['stg'] % 3
                            cnt['stg'] += 1
                            dma('sp', zt[zi][:, 0:T], PT[l][zoff + fcd * 128:zoff + (fcd + 1) * 128, s0:s0 + T], [],
                                [('zt', zi)])
                            stt(YP[:, fcd, co:co + T], ps[:, 0:T], spTs[:, fcd:fcd + 1], zt[zi][:, 0:T], ALU.mult, ALU.mult,
                                [pr, 'spTs', ('zt', zi)], [('YP', fcd, si)])
                    for fc in range(HC):
                        zi = cnt['stg'] % 3
                        cnt['stg'] += 1
                        dma('sp', zt[zi][:, 0:T], PT[l][zaoff + fc * 128:zaoff + (fc + 1) * 128, s0:s0 + T], [],
                            [('zt', zi)])
                        dma('sp', at_[zi][:, 0:T], AOT[l][fc * 128:(fc + 1) * 128, s0:s0 + T], [], [('at', zi)])
                        tt(YA[:, fc, co:co + T], at_[zi][:, 0:T], zt[zi][:, 0:T], ALU.mult, [('at', zi), ('zt', zi)],
                           [('YA', fc, si)], eng='pool')
                for dc in range(KC):
                    wp, wpr = wload(w_brp_t[l, dc], HC * 128)
                    wa, war = wload(w_bra_t[l, dc], HC * 128)
                    for si, (s0, T, isctx) in enumerate(stl):
                        co = si * 512
                        ps1, pr1 = nextps()
                        for c in range(HC):
                            mm(ps1[:, 0:T], wp[:, c * 128:(c + 1) * 128], YP[:, c, co:co + T], c == 0, c == HC - 1,
                               wpr + [('YP', c, si)], [pr1])
                        ps2, pr2 = nextps()
                        for c in range(HC):
                            mm(ps2[:, 0:T], wa[:, c * 128:(c + 1) * 128], YA[:, c, co:co + T], c == 0, c == HC - 1,
                               war + [('YA', c, si)], [pr2])
                        gi = cnt['x'] % 2
                        cnt['x'] += 1
                        dma('sp', gpt[gi][:, 0:T], PT[l][gpoff + dc * 128:gpoff + (dc + 1) * 128, s0:s0 + T], [],
                            [('gpt', gi)])
                        dma('sp', gat[gi][:, 0:T], PT[l][gaoff + dc * 128:gaoff + (dc + 1) * 128, s0:s0 + T], [],
                            [('gat', gi)])
                        tt(t1[gi][:, 0:T], ps1[:, 0:T], gpt[gi][:, 0:T], ALU.mult, [pr1, ('gpt', gi)], [('t1', gi)])
                        tt(t2[gi][:, 0:T], ps2[:, 0:T], gat[gi][:, 0:T], ALU.mult, [pr2, ('gat', gi)], [('t2', gi)])
                        tt(mts[gi][:, 0:T], t1[gi][:, 0:T], t2[gi][:, 0:T], ALU.add, [('t1', gi), ('t2', gi)],
                           [('mts', gi)])
                        dma('act', MTD[l][dc * 128:(dc + 1) * 128, s0:s0 + T], mts[gi][:, 0:T], [('mts', gi)],
                            [('MTD', l, dc, s0)])
                    ada_pop()
                P.barrier()
                for si, (s0, T, isctx) in enumerate(stl):
                    co = si * 512
                    dma('sp', MT[:, :, co:co + T], MTD[l][:, s0:s0 + T].rearrange("(kc p) s -> p kc s", p=128), [],
                        [('MT', si)])
                def epilogue(fg, stl=stl):
                    for si, (s0, T, isctx) in enumerate(stl):
                        if isctx:
                            xs_ap, xd_ap, xrow0 = ctxb, c1, 0
                        else:
                            xs_ap = xw if l == 0 else x1
                            xd_ap = x1 if l == 0 else x2
                            xrow0 = s0
                        for jt in range((T + 127) // 128):
                            tn = min(128, T - jt * 128)
                            pi = 6 + (cnt['bt'] % 2)
                            cnt['bt'] += 1
                            for q4 in range(4):
                                P.add('pe', lambda e, q4=q4, jt=jt, pi=pi, tn=tn, si=si: e.transpose(
                                    out=psf[pi][0:tn, q4 * 128:(q4 + 1) * 128], in_=tT[si][q4][:, jt * 128:jt * 128 + tn],
                                    identity=identf[:]), reads=[('tT', si, q4), 'identf'], writes=[('ps', pi)], same=False)
                            xi = cnt['x'] % 3
                            cnt['x'] += 1
                            row = xrow0 + jt * 128
                            dma('sp', xr[xi][0:tn, :], xs_ap[row:row + tn, fg * 512:(fg + 1) * 512], [], [('xr', xi)])
                            tt(xn[xi][0:tn, :], psf[pi][0:tn, :], xr[xi][0:tn, :], ALU.add, [('ps', pi), ('xr', xi)],
                               [('xn', xi)])
                            dma('act', xd_ap[row:row + tn, fg * 512:(fg + 1) * 512], xn[xi][0:tn, :], [('xn', xi)],
                                [('xd', l, isctx, row, fg)])

                pend_epi = None
                for fg in range(KC // 4):
                    for q4 in range(4):
                        dc2 = fg * 4 + q4
                        wo, wor = wload(w_out_t[l, dc2], KC * 128)
                        pss = []
                        for si, (s0, T, isctx) in enumerate(stl):
                            co = si * 512
                            ps, pr = nextps()
                            for kc in range(KC):
                                mm(ps[:, 0:T], wo[:, kc * 128:(kc + 1) * 128], MT[:, kc, co:co + T], kc == 0, kc == KC - 1,
                                   wor + [('MT', si)], [pr])
                            pss.append((ps, pr))
                        if q4 == 0 and pend_epi is not None:
                            epilogue(pend_epi)
                            pend_epi = None
                        for si, (s0, T, isctx) in enumerate(stl):
                            ps, pr = pss[si]
                            tsc(tT[si][q4][:, 0:T], ps[:, 0:T], Gv(isctx, dc2), None, ALU.mult, None,
                                [pr, ('adag', lp, isctx)], [('tT', si, q4)])
                        ada_pop()
                    pend_epi = fg
                epilogue(pend_epi)

        P.barrier()
        dma('sp', fgs[:], fg_rep, [], ['fgs'])
        for jt in range(8):
            xi = jt % 2
            sc0 = 4 * xi
            xh = xhb[xi]
            row = 512 + jt * 128
            dma('sp', xt[xi][:], x2[row:row + 128, :], [], [('xt', xi)])
            act(xh[:], xt[xi][:], AF.Square, [('xt', xi)], [('xh', xi), ('ss0', xi)], accum_out=ss[:, sc0:sc0 + 1])
            tsc(ss[:, sc0 + 1:sc0 + 2], ss[:, sc0:sc0 + 1], 1.0 / D, 1e-6, ALU.mult, ALU.add, [('ss0', xi)], [('ss1', xi)])
            act(ss[:, sc0 + 2:sc0 + 3], ss[:, sc0 + 1:sc0 + 2], AF.Sqrt, [('ss1', xi)], [('ss2', xi)])
            P.add('dve', lambda e, sc0=sc0: e.reciprocal(out=ss[:, sc0 + 3:sc0 + 4], in_=ss[:, sc0 + 2:sc0 + 3]),
                  reads=[('ss2', xi)], writes=[('ss3', xi)])
            stt(xt[xi][:], xt[xi][:], ss[:, sc0 + 3:sc0 + 4], fgs[:], ALU.mult, ALU.mult, [('xt', xi), ('ss3', xi), 'fgs'],
                [('xt', xi)])
            dma('act', y[jt * 128:(jt + 1) * 128, :], xt[xi][:], [('xt', xi)], [('y', jt)])
        P.emit()
        build.stats = dict(nops={e: len(P.ops[e]) for e in ENGS}, maxcnt=P.maxcnt)
    return nc


def _tile_w(w, kc, ncb):
    K, C = w.shape
    return np.ascontiguousarray(w.reshape(kc, 128, ncb, 128).transpose(2, 1, 0, 3)).reshape(ncb, 128, kc * 128)


def _fm(v):
    return np.ascontiguousarray(v.reshape(-1, 128).T)


def _base_tables(rpb_l, NH):
    kc = np.arange(64)[:, None]
    qc = np.arange(64)[None, :]
    ws = np.clip(qc - 8, 0, 48)
    colvalid = (kc >= ws) & (kc < ws + 16)
    dc = np.clip(kc - qc + 15, 0, 30)
    base = np.where(colvalid[None, None], rpb_l[:, :, dc], np.float32(NEG)).astype(np.float32)
    return base


def _host_prep(inputs, D):
    L = inputs["w_in"].shape[0]
    KC = D // 128
    HC = KC // 2
    NH = HC
    GP = D // 8
    GC = GP // 128
    f32 = np.float32
    shared = {}
    shared["ident"] = np.eye(128, dtype=f32)
    shared["w_in_t"] = np.stack([_tile_w(np.asarray(inputs["w_in"][l], f32), KC, 5 * KC) for l in range(L)])
    shared["w_ada_t"] = np.stack([_tile_w(np.asarray(inputs["w_ada"][l], f32), KC, 3 * KC) for l in range(L)])
    shared["w_out_t"] = np.stack([_tile_w(np.asarray(inputs["w_out"][l], f32), KC, KC) for l in range(L)])
    shared["w_brp_t"] = np.stack([_tile_w(np.asarray(inputs["w_br_pool"][l], f32), HC, KC) for l in range(L)])
    shared["w_bra_t"] = np.stack([_tile_w(np.asarray(inputs["w_br_attn"][l], f32), HC, KC) for l in range(L)])
    wp = np.asarray(inputs["w_pool"], f32)
    shared["w_pool_t"] = np.ascontiguousarray(wp.reshape(L, 4, GC, 128, GP).transpose(0, 1, 3, 2, 4)).reshape(L, 4, 128, GC * GP)
    shared["b_in_T"] = np.stack([_fm(np.asarray(inputs["b_in"][l], f32)) for l in range(L)])
    shared["b_ada_T"] = np.stack([_fm(np.asarray(inputs["b_ada"][l], f32)) for l in range(L)])
    shared["g_T"] = np.stack([_fm(np.asarray(inputs["norm_g"][l], f32)) for l in range(L)])
    shared["sp_T"] = np.stack([_fm(np.asarray(inputs["s_pool"][l], f32)) for l in range(L)])
    shared["fg_rep"] = np.ascontiguousarray(np.broadcast_to(np.asarray(inputs["final_g"], f32)[None, :], (128, D)))
    rpb = np.asarray(inputs["rpb"], f32)
    bases = [_base_tables(rpb[l], NH) for l in range(L)]
    negblk = np.full((NH, 64, 64), NEG, f32)

    def base_dr(l, dr):
        if -7 <= dr <= 7:
            return bases[l][:, dr + 7]
        return negblk

    treg = np.empty((L, NH, 128, 9, 64), f32)
    for l in range(L):
        for i in range(9):
            dlo = 3 - i
            dhi = 4 - i
            treg[l, :, 0:64, i, :] = base_dr(l, dlo) if -4 <= dlo <= 3 else negblk
            treg[l, :, 64:128, i, :] = base_dr(l, dhi) if -4 <= dhi <= 3 else negblk
    shared["treg"] = treg.reshape(L, NH, 128, 9 * 64)
    invc = np.zeros((128, 64), f32)
    for g, w in enumerate(POOL_WINDOWS):
        for k, t in enumerate(list(range(0, 8)) + list(range(248, 256))):
            lo = np.clip(t - w // 2, 0, 255)
            hi = np.clip(t - w // 2 + w - 1, 0, 255)
            invc[:, g * 16 + k] = 1.0 / (hi - lo + 1)
    shared["invc"] = invc

    x = np.asarray(inputs["x"], f32)
    ctx = np.asarray(inputs["ctx"], f32)
    c = np.asarray(inputs["c"], f32)
    c_ctx = np.asarray(inputs["c_ctx"], f32)
    maps = []
    for core in range(8):
        b, j = core // 4, core % 4
        R0 = 16 * j - 8
        m = dict(shared)
        xwin = np.zeros((30, 64, D), f32)
        rows = np.arange(R0, R0 + 30)
        ok = (rows >= 0) & (rows < 64)
        xg = x[b].reshape(64, 64, D)
        xwin[ok] = xg[rows[ok]]
        m["xw"] = xwin.reshape(NLAT, D)
        m["ctxb"] = np.ascontiguousarray(ctx[b])
        cT = np.empty((128, KC, 2), f32)
        cT[:, :, 0] = _fm(c[b])
        cT[:, :, 1] = _fm(c_ctx)
        m["cT"] = cT.reshape(128, KC * 2)
        vm = np.zeros((128, NLAT + 16), f32)
        tokvalid = np.repeat(ok, 64).astype(f32)
        vm[:, 8:8 + NLAT] = tokvalid[None, :]
        m["vm"] = vm
        invl = np.zeros((128, 64), f32)
        for g, w in enumerate(POOL_WINDOWS):
            for k in range(16):
                slot = (512 + k) if k < 8 else (1528 + k - 8)
                t = R0 * 64 + slot
                if 0 <= t < 4096:
                    lo = np.clip(t - w // 2, 0, 4095)
                    hi = np.clip(t - w // 2 + w - 1, 0, 4095)
                    invl[:, g * 16 + k] = 1.0 / (hi - lo + 1)
                else:
                    invl[:, g * 16 + k] = 1.0 / w
        m["invl"] = invl
        ts = np.empty((L, NH, 128, 7, 6, 64), f32)
        for si, r in enumerate(SPEC_ROWS):
            gr = R0 + r
            abase = 4 if r <= 11 else 16
            if 0 <= gr < 64:
                rs = int(np.clip(gr - 4, 0, 56))
                vrows = set(range(rs, rs + 8))
            else:
                vrows = None
            for ci in range(6):
                for half in range(2):
                    kr = abase + 2 * ci + half
                    gk = R0 + kr
                    dr = kr - r
                    if vrows is None:
                        valid = -4 <= dr <= 3
                    else:
                        valid = gk in vrows
                    for l in range(L):
                        ts[l, :, half * 64:(half + 1) * 64, si, ci, :] = base_dr(l, dr) if valid else negblk
        m["tspec"] = ts.reshape(L, NH, 128, 7 * 6 * 64)
        maps.append(m)
    return maps


_CACHE = {}


def kernel(**inputs):
    D = int(inputs["x"].shape[-1])
    if D not in _CACHE:
        _CACHE[D] = build(D, L=int(inputs["w_in"].shape[0]))
    nc = _CACHE[D]
    maps = _host_prep(inputs, D)
    res = run_bass_kernel_spmd(nc, maps, core_ids=list(range(8)))
    out = np.empty((2, 4096, D), np.float32)
    for core in range(8):
        b, j = core // 4, core % 4
        out[b, j * 1024:(j + 1) * 1024, :] = np.asarray(res.results[core]["y"], np.float32)
    return out
```

```python
import numpy as np
import contextlib
import concourse.bass as bass
import concourse.mybir as mybir
from concourse.bass_utils import run_bass_kernel_spmd

F32 = mybir.dt.float32
BF16 = mybir.dt.bfloat16
AF = mybir.ActivationFunctionType
ALU = mybir.AluOpType

ENGS = ['pe', 'act', 'dve', 'pool', 'sp']
BLOCKNAME = {'pe': 'tensor', 'act': 'scalar', 'dve': 'vector', 'pool': 'gpsimd', 'sp': 'sync'}
NEG = -1e30
POOL_WINDOWS = (2, 4, 8, 16)
NLAT = 1920
NSLOT = 2176
SPEC_ROWS = (8, 9, 10, 11, 21, 22, 23)
DEBUG = False


class Op:
    __slots__ = ('eng', 'fn', 'deps', 'dma', 'ndep', 'sem', 'val', 'gidx')

    def __init__(self, eng, fn, dma):
        self.eng = eng
        self.fn = fn
        self.dma = dma
        self.deps = []
        self.ndep = 0
        self.sem = None
        self.val = 0


class Prog:
    def __init__(self, nc, stack, ndma=8):
        self.nc = nc
        self.stack = stack
        self.ops = {e: [] for e in ENGS}
        self.lastw = {}
        self.readers = {}
        self.ndma = ndma
        self.nops = 0
        self.all_dma = []

    def add(self, eng, fn, reads=(), writes=(), dma=False, same=True, extra=()):
        op = Op(eng, fn, dma)
        op.gidx = self.nops
        self.nops += 1
        deps = {}
        for r in reads:
            for w in self.lastw.get(r, ()):
                deps[id(w)] = w
        for r in writes:
            for w in self.lastw.get(r, ()):
                deps[id(w)] = w
            for rd in self.readers.get(r, ()):
                deps[id(rd)] = rd
        for d in extra:
            deps[id(d)] = d
        for d in deps.values():
            if d is op:
                continue
            if (not same) and (not d.dma) and d.eng == eng:
                continue
            op.deps.append(d)
            d.ndep += 1
        for r in writes:
            self.lastw[r] = [op]
            self.readers[r] = []
        for r in reads:
            lst = self.readers.setdefault(r, [])
            if not dma:
                for i, o in enumerate(lst):
                    if (not o.dma) and o.eng == eng:
                        lst[i] = op
                        break
                else:
                    lst.append(op)
            else:
                lst.append(op)
        self.ops[eng].append(op)
        if dma:
            self.all_dma.append(op)
        return op

    def barrier(self, engines=('pe', 'act', 'dve', 'sp'), dma_queues=('sp', 'act')):
        deps = []
        for e in ('pe', 'act', 'dve'):
            for o in reversed(self.ops[e]):
                if o.fn is not None:
                    deps.append(o)
                    break
        for q in dma_queues:
            dl = [o for o in self.ops[q] if o.dma]
            deps.extend(dl[-self.ndma:])
        for e in engines:
            self.add(e, None, extra=[d for d in deps])

    def emit(self):
        nc = self.nc
        engsem = {}
        for e in ENGS:
            engsem[e] = self.stack.enter_context(nc.semaphore(f"s_{e}"))
        dmasem = {}
        for e in ENGS:
            nd = sum(1 for o in self.ops[e] if o.dma)
            if nd:
                dmasem[e] = [self.stack.enter_context(nc.semaphore(f"d_{e}{i}")) for i in range(min(self.ndma, nd))]
        self.maxcnt = {}
        for e in ENGS:
            cnt = 0
            nd = 0
            K = len(dmasem.get(e, []))
            for op in self.ops[e]:
                if op.fn is None:
                    continue
                if op.dma:
                    op.sem = dmasem[e][nd % K]
                    op.val = 16 * (nd // K + 1)
                    nd += 1
                elif op.ndep > 0:
                    cnt += 1
                    op.sem = engsem[e]
                    op.val = cnt
            self.maxcnt[e] = cnt
        final_dma = {}
        for op in self.all_dma:
            final_dma[id(op.sem)] = (op.sem, max(op.val, final_dma.get(id(op.sem), (None, 0))[1]))
        with nc.Block() as blk:
            for e in ENGS:
                ops = self.ops[e]
                if not ops and e != 'sp':
                    continue

                def body(engine, e=e, ops=ops):
                    waited = {}

                    def wait(sem, val):
                        if waited.get(id(sem), 0) >= val:
                            return
                        engine.wait_ge(sem, val)
                        waited[id(sem)] = val

                    for op in ops:
                        for d in sorted(op.deps, key=lambda o: o.gidx):
                            wait(d.sem, d.val)
                        if op.fn is None:
                            continue
                        if op.dma and op.val > 16:
                            wait(op.sem, op.val - 16)
                        ins = op.fn(engine)
                        if op.dma:
                            ins.then_inc(op.sem, 16)
                        elif op.sem is not None:
                            ins.then_inc(op.sem, 1)
                    if e == 'sp':
                        for sem, val in final_dma.values():
                            wait(sem, val)

                getattr(blk, BLOCKNAME[e])(body)


def split_tiles(a, b, step):
    out = []
    while a < b:
        n = min(step, b - a)
        out.append((a, n))
        a += n
    return out


def build(D, L=2):
    KC = D // 128
    HC = KC // 2
    DP = D // 2
    NH = HC
    GP = D // 8
    GC = GP // 128
    NCB = 5 * KC
    NBUF = 1728
    nc = bass.Bass("TRN2", target_bir_lowering=False)

    def din(name, shape, dt=F32):
        return nc.dram_tensor(name, shape, dt, kind="ExternalInput").ap()

    xw = din("xw", [NLAT, D])
    ctxb = din("ctxb", [256, D])
    cT = din("cT", [128, KC * 2])
    ident_in = din("ident", [128, 128])
    w_in_t = din("w_in_t", [L, NCB, 128, KC * 128])
    w_ada_t = din("w_ada_t", [L, 3 * KC, 128, KC * 128])
    w_out_t = din("w_out_t", [L, KC, 128, KC * 128])
    w_brp_t = din("w_brp_t", [L, KC, 128, HC * 128])
    w_bra_t = din("w_bra_t", [L, KC, 128, HC * 128])
    w_pool_t = din("w_pool_t", [L, 4, 128, GC * GP])
    b_in_T = din("b_in_T", [L, 128, NCB])
    b_ada_T = din("b_ada_T", [L, 128, 3 * KC])
    g_T = din("g_T", [L, 128, KC])
    sp_T = din("sp_T", [L, 128, HC])
    fg_rep = din("fg_rep", [128, D])
    treg = din("treg", [L, NH, 128, 9 * 64])
    tspec = din("tspec", [L, NH, 128, 7 * 6 * 64])
    vm_in = din("vm", [128, NLAT + 16])
    invl_in = din("invl", [128, 64])
    invc_in = din("invc", [128, 64])
    y = nc.dram_tensor("y", [1024, D], F32, kind="ExternalOutput").ap()

    def dscr(name, shape, dt):
        return nc.dram_tensor(name, shape, dt, kind=("ExternalOutput" if DEBUG else "Internal")).ap()

    PT = [dscr(f"PT{l}", [5 * D, NSLOT], BF16) for l in range(L)]
    AOT = [dscr(f"AOT{l}", [DP, NSLOT], BF16) for l in range(L)]
    x1 = dscr("x1", [NLAT, D], F32)
    c1 = dscr("c1", [256, D], F32)
    x2 = dscr("x2", [NLAT, D], F32)
    MTD = [dscr(f"MTD{l}", [D, NSLOT], BF16) for l in range(L)]

    base = [nc.sbuf_base]

    def sb_at(name, shape, dt, off):
        return nc.alloc_sbuf_tensor_at(name, shape, dt, offset=off)

    cur = [((nc.sbuf_base + 63) // 64) * 64]

    def sb(name, shape, dt):
        esz = 4 if dt == F32 else 2
        n = 1
        for s in shape[1:]:
            n *= s
        nbytes = ((n * esz + 63) // 64) * 64
        t = sb_at(name, shape, dt, cur[0])
        cur[0] += nbytes
        return t

    identf = sb("identf", [128, 128], F32)
    identb = sb("identb", [128, 128], BF16)
    onesb = sb("onesb", [128, 128], BF16)
    cTs = sb("cTs", [128, KC * 2], F32)
    scin = sb("scin", [128, KC, 2], BF16)
    bT = sb("bT", [128, NCB], F32)
    bqs = sb("bqs", [128, HC], F32)
    badaTL = [sb(f"badaT{i}", [128, 3 * KC], F32) for i in range(2)]
    gTsL = [sb(f"gTs{i}", [128, KC], F32) for i in range(2)]
    spTs = sb("spTs", [128, HC], F32)
    adaL = [sb(f"ada{i}", [128, 3 * KC, 2], F32) for i in range(2)]
    AtabL = [sb(f"Atab{i}", [128, 2, KC], F32) for i in range(2)]
    vm = sb("vm", [128, NLAT + 16], F32)
    invl = sb("invl", [128, 64], F32)
    invc = sb("invc", [128, 64], F32)
    ss = sb("ss", [128, 8], F32)
    zb = sb("zb", [128, 2 * HC, 64], BF16)
    stg = [sb(f"stg{i}", [128, 512], BF16) for i in range(4)]
    NW = 3
    wsl = [sb(f"wsl{i}", [128, KC * 128], BF16) for i in range(NW)]
    a1 = cur[0]
    hT = sb("hT", [128, KC, NBUF], BF16)
    a1_end = cur[0]
    a2 = cur[0]
    xt = [sb(f"xt{i}", [128, D], F32) for i in range(2)]
    xhb = [sb(f"xh{i}", [128, D], BF16) for i in range(2)]
    a2_end = cur[0]
    assert cur[0] <= nc.SBUF_PARTITION_SIZE_BYTES - 64, cur[0]

    cur[0] = a1
    KTh = [sb(f"KTh{i}", [128, NSLOT], BF16) for i in range(2)]
    VTh = [sb(f"VTh{i}", [128, NSLOT], BF16) for i in range(2)]
    QTh = [sb(f"QTh{i}", [128, NSLOT], BF16) for i in range(2)]
    Vh = [sb(f"Vh{i}", [128, 17, 128], BF16) for i in range(2)]
    tregs = [sb(f"tregs{i}", [128, 9, 64], F32) for i in range(2)]
    tspecs = [sb(f"tspecs{i}", [128, 7, 6, 64], F32) for i in range(2)]
    tregb = [sb(f"tregb{i}", [128, 9, 64], BF16) for i in range(2)]
    tspecb = [sb(f"tspecb{i}", [128, 7, 6, 64], BF16) for i in range(2)]
    sbf = [sb(f"sbf{i}", [128, 512], F32) for i in range(2)]
    pTb = [sb(f"pTb{i}", [128, 512], BF16) for i in range(4)]
    rsb = [sb(f"rsb{i}", [128, 512], F32) for i in range(2)]
    ostg = [sb(f"ostg{i}", [128, 512], BF16) for i in range(2)]
    assert cur[0] <= a1_end
    cur[0] = a1
    MT = sb("MT", [128, KC, 1024], BF16)
    cur[0] = a1
    YP = sb("YP", [128, HC, 1024], BF16)
    YA = sb("YA", [128, HC, 1024], BF16)
    mts = [sb(f"mts{i}", [128, 512], BF16) for i in range(2)]
    wpool = sb("wpool", [128, 4, GC, GP], BF16)
    ub = [sb(f"ub{i}", [128, 528], BF16) for i in range(2)]
    um = [sb(f"um{i}", [128, 528], F32) for i in range(2)]
    pa = sb("pa", [128, 528], F32)
    pb = sb("pb", [128, 528], F32)
    pTg = [sb(f"pTg{i}", [128, GC, 512], BF16) for i in range(2)]
    zt = [sb(f"zt{i}", [128, 512], BF16) for i in range(3)]
    at_ = [sb(f"at{i}", [128, 512], BF16) for i in range(3)]
    assert cur[0] <= a1_end, (cur[0], a1_end)
    cur[0] = a1
    fgs = sb("fgs", [128, D], F32)
    cur[0] = a2
    tT = [[sb(f"tT{j}_{i}", [128, 512], F32) for i in range(4)] for j in range(2)]
    xr = [sb(f"xr{i}", [128, 512], F32) for i in range(3)]
    xn = [sb(f"xn{i}", [128, 512], F32) for i in range(3)]
    t1 = [sb(f"t1{i}", [128, 512], F32) for i in range(2)]
    t2 = [sb(f"t2{i}", [128, 512], F32) for i in range(2)]
    gpt = [sb(f"gpt{i}", [128, 512], BF16) for i in range(2)]
    gat = [sb(f"gat{i}", [128, 512], BF16) for i in range(2)]
    pa2 = sb("pa2", [128, 528], F32)
    pb2 = sb("pb2", [128, 528], F32)
    assert cur[0] <= a2_end, (cur[0], a2_end)

    psf = [nc.alloc_psum_tensor(f"psf{i}", [128, 512], F32) for i in range(8)]
    psb6 = psf[6].ap().bitcast(BF16)
    psb7 = psf[7].ap().bitcast(BF16)
    psb = [psb6, psb7]

    with contextlib.ExitStack() as st:
        P = Prog(nc, st, ndma=8)
        cnt = {'ps': 0, 'w': 0, 'stg': 0, 'x': 0, 'bt': 0}

        def dma(q, out, in_, reads, writes, **kw):
            return P.add(q, lambda e: e.dma_start(out=out, in_=in_, **kw), reads=reads, writes=writes, dma=True)

        def wload(src_ap, ncols):
            half = ncols <= HC * 128
            hptr = cnt['w'] % (2 * NW)
            if (not half) and (hptr % 2 == 1):
                cnt['w'] += 1
                hptr = cnt['w'] % (2 * NW)
            i, j = hptr // 2, hptr % 2
            if half:
                cnt['w'] += 1
                dst = wsl[i][:, j * HC * 128:j * HC * 128 + ncols]
                res = [('wh', i, j)]
            else:
                cnt['w'] += 2
                dst = wsl[i][:, 0:ncols]
                res = [('wh', i, 0), ('wh', i, 1)]
            P.add('pool', lambda e: e.dma_start(out=dst, in_=src_ap, max_dma_last_dim=4096), reads=(), writes=res, dma=True)
            return dst, res

        def nextps():
            i = cnt['ps'] % 4
            cnt['ps'] += 1
            return psf[i], ('ps', i)

        def mm(out, lhsT, rhs, start, stop, reads, writes, skip=False):
            if skip:
                P.add('pe', lambda e: e.matmul(out, lhsT, rhs, start=start, stop=stop, skip_group_check=True),
                      reads=reads, writes=writes, same=False)
            else:
                P.add('pe', lambda e: e.matmul(out, lhsT, rhs, start=start, stop=stop),
                      reads=reads, writes=writes, same=False)

        def act(out, in_, func, reads, writes, bias=None, scale=None, accum_out=None):
            kw = {}
            if bias is not None:
                kw['bias'] = bias
            if scale is not None:
                kw['scale'] = scale
            if accum_out is not None:
                kw['accum_out'] = accum_out
            P.add('act', lambda e: e.activation(out=out, in_=in_, func=func, **kw), reads=reads, writes=writes)

        def tt(out, in0, in1, op, reads, writes, eng='dve'):
            P.add(eng, lambda e: e.tensor_tensor(out=out, in0=in0, in1=in1, op=op), reads=reads, writes=writes)

        def tsc(out, in0, s1, s2, op0, op1, reads, writes):
            if op1 is None:
                P.add('dve', lambda e: e.tensor_scalar(out=out, in0=in0, scalar1=s1, scalar2=None, op0=op0),
                      reads=reads, writes=writes)
            else:
                P.add('dve', lambda e: e.tensor_scalar(out=out, in0=in0, scalar1=s1, scalar2=s2, op0=op0, op1=op1),
                      reads=reads, writes=writes)

        def stt(out, in0, scalar, in1, op0, op1, reads, writes):
            P.add('dve', lambda e: e.scalar_tensor_tensor(out=out, in0=in0, scalar=scalar, in1=in1, op0=op0, op1=op1),
                  reads=reads, writes=writes)

        def cp(out, in_, reads, writes, eng='dve'):
            P.add(eng, lambda e: e.tensor_copy(out=out, in_=in_), reads=reads, writes=writes)

        dma('sp', identf[:], ident_in, [], ['identf'])
        dma('sp', cTs[:], cT, [], ['cTs'])
        dma('sp', vm[:], vm_in, [], ['vm'])
        dma('sp', invl[:], invl_in, [], ['invl'])
        dma('sp', invc[:], invc_in, [], ['invc'])
        cp(identb[:], identf[:], ['identf'], ['identb'])
        P.add('dve', lambda e: e.memset(onesb[:], 1.0), writes=['onesb'])
        act(scin[:].rearrange("p k v -> p (k v)"), cTs[:], AF.Silu, ['cTs'], ['scin'])
        P.add('dve', lambda e: e.memset(zb[:], 0.0), writes=['zb'])
        for l_ in range(1, L):
            dma('sp', PT[l_][3 * DP:5 * DP, 1728:1792].rearrange("(c p) s -> p c s", p=128), zb[:], ['zb'], [('PTz', l_)])

        for l in range(L):
            dma('sp', bT[:], b_in_T[l], [], ['bT'])
            dma('sp', spTs[:], sp_T[l], [], ['spTs'])
            tsc(bqs[:], bT[:, 2 * HC:3 * HC], float(128 ** -0.5), None, ALU.mult, None, ['bT'], ['bqs'])

            def make_ada_steps(l2):
                ada_, Atab_, badaT_, gTs_ = adaL[l2 % 2], AtabL[l2 % 2], badaTL[l2 % 2], gTsL[l2 % 2]
                psA, rA = psf[4], ('ps', 4)
                lst = []

                def loads():
                    dma('sp', badaT_[:], b_ada_T[l2], [], [('badaT', l2 % 2)])
                    dma('sp', gTs_[:], g_T[l2], [], [('gTs', l2 % 2)])
                lst.append(loads)
                for cb in range(3 * KC):
                    def step(cb=cb):
                        wt, wr = wload(w_ada_t[l2, cb], KC * 128)
                        for kc in range(KC):
                            mm(psA[:, 2 * cb:2 * cb + 2], wt[:, kc * 128:(kc + 1) * 128], scin[:, kc, :], kc == 0,
                               kc == KC - 1, wr + ['scin'], [rA], skip=True)
                    lst.append(step)

                def fin_a():
                    for v in range(2):
                        tt(ada_[:, 0:2 * KC, v], psA[:, 0:4 * KC].rearrange("p (c v) -> p c v", v=2)[:, :, v],
                           badaT_[:, 0:2 * KC], ALU.add, [rA, ('badaT', l2 % 2)], [('ada', l2 % 2, v)])
                        stt(Atab_[:, v, :], ada_[:, KC:2 * KC, v], 1.0, gTs_[:], ALU.add, ALU.mult,
                            [('ada', l2 % 2, v), ('gTs', l2 % 2)], [('A', l2 % 2, v)])

                def fin_b():
                    for v in range(2):
                        tt(ada_[:, 2 * KC:3 * KC, v], psA[:, 4 * KC:6 * KC].rearrange("p (c v) -> p c v", v=2)[:, :, v],
                           badaT_[:, 2 * KC:3 * KC], ALU.add, [rA, ('badaT', l2 % 2)], [('adag', l2 % 2, v)])
                lst.insert(1 + 2 * KC, fin_a)
                lst.append(fin_b)
                return lst

            def ada_pop():
                if pending_ada:
                    pending_ada.pop(0)()

            if l == 0:
                steps0 = make_ada_steps(0)
                for f_ in steps0[:2 + 2 * KC]:
                    f_()
                pending_ada = steps0[2 + 2 * KC:]
                if L > 1:
                    pending_ada = pending_ada + make_ada_steps(1)
            else:
                for f_ in pending_ada:
                    f_()
                pending_ada = []
                if l + 1 < L:
                    pending_ada = make_ada_steps(l + 1)
            ada, Atab = adaL[l % 2], AtabL[l % 2]
            lp = l % 2

            if DEBUG:
                dbg = dscr(f"dbg_ada{l}", [128, 3 * KC * 2], F32)
                dma('sp', dbg, ada[:].rearrange("p c v -> p (c v)"), [('ada', lp, 0), ('ada', lp, 1)], [('dbgada', l)])

            def Bv(v, kc):
                return ada[:, kc, v:v + 1]

            def Gv(v, kc):
                return ada[:, 2 * KC + kc, v:v + 1]

            if l == 0:
                passes = [
                    dict(segs=[(256, 1472, 0), (1920, 256, 1)],
                         fam_tiles=None),
                    dict(segs=[(0, 256, 0), (1728, 192, 0)], fam_tiles='ukv'),
                ]
            else:
                passes = [dict(segs=[(256, 1472, 0), (1920, 256, 1)], fam_tiles='l1')]
            xsrc_lat = xw if l == 0 else x1
            xsrc_ctx = ctxb if l == 0 else c1
            for ps_ in passes:
                P.barrier()
                boff = 0
                seginfo = []
                for (s0, n, isctx) in ps_['segs']:
                    seginfo.append((s0, n, isctx, boff))
                    for (ts, tn) in split_tiles(0, n, 128):
                        xi = cnt['x'] % 2
                        cnt['x'] += 1
                        xh = xhb[xi]
                        sc0 = 4 * xi
                        src = (xsrc_ctx[ts:ts + tn, :] if isctx else xsrc_lat[s0 + ts:s0 + ts + tn, :])
                        dma('sp', xt[xi][0:tn, :], src, [], [('xt', xi)])
                        act(xh[0:tn, :], xt[xi][0:tn, :], AF.Square, [('xt', xi)], [('xh', xi), ('ss0', xi)],
                            accum_out=ss[0:tn, sc0:sc0 + 1])
                        tsc(ss[0:tn, sc0 + 1:sc0 + 2], ss[0:tn, sc0:sc0 + 1], 1.0 / D, 1e-6, ALU.mult, ALU.add, [('ss0', xi)],
                            [('ss1', xi)])
                        act(ss[0:tn, sc0 + 2:sc0 + 3], ss[0:tn, sc0 + 1:sc0 + 2], AF.Sqrt, [('ss1', xi)], [('ss2', xi)])
                        P.add('dve', lambda e, tn=tn, sc0=sc0: e.reciprocal(out=ss[0:tn, sc0 + 3:sc0 + 4],
                                                                          in_=ss[0:tn, sc0 + 2:sc0 + 3]),
                              reads=[('ss2', xi)], writes=[('ss3', xi)])
                        tsc(xh[0:tn, :], xt[xi][0:tn, :], ss[0:tn, sc0 + 3:sc0 + 4], None, ALU.mult, None,
                            [('xt', xi), ('ss3', xi)], [('xh', xi)])
                        bpos = boff + ts
                        for k8 in range(0, KC, 8):
                            pi = cnt['bt'] % 2
                            cnt['bt'] += 1
                            for kk in range(8):
                                kc = k8 + kk
                                P.add('pe', lambda e, kc=kc, kk=kk, pi=pi, tn=tn, xh=xh: e.transpose(
                                    out=psb[pi][:, kk * 128:kk * 128 + tn], in_=xh[0:tn, kc * 128:(kc + 1) * 128],
                                    identity=identb[0:tn, 0:tn]), reads=[('xh', xi), 'identb'], writes=[('ps', 6 + pi)],
                                    same=False)
                            for kk in range(8):
                                kc = k8 + kk
                                if kk % 2 == 0:
                                    act(hT[:, kc, bpos:bpos + tn], psb[pi][:, kk * 128:kk * 128 + tn], AF.Identity,
                                        [('ps', 6 + pi), ('A', lp, isctx), ('ada', lp, isctx)], [('hT', bpos // 128)],
                                        bias=Bv(isctx, kc), scale=Atab[:, isctx, kc:kc + 1])
                                else:
                                    tsc(hT[:, kc, bpos:bpos + tn], psb[pi][:, kk * 128:kk * 128 + tn],
                                        Atab[:, isctx, kc:kc + 1], Bv(isctx, kc), ALU.mult, ALU.add,
                                        [('ps', 6 + pi), ('A', lp, isctx), ('ada', lp, isctx)], [('hT', bpos // 128)])
                    boff += n

                def tiles_for(fam):
                    res = []
                    ft = ps_['fam_tiles']
                    for (s0, n, isctx, bo) in seginfo:
                        lo, hi = 0, n
                        if ft == 'l1':
                            if isctx:
                                if fam not in ('k', 'v'):
                                    continue
                            elif fam == 'u':
                                lo, hi = (7 - 4) * 64, (25 - 4) * 64
                            elif fam not in ('k', 'v'):
                                lo, hi = (8 - 4) * 64, (24 - 4) * 64
                        elif ft == 'ukv':
                            if fam not in ('u', 'k', 'v'):
                                continue
                        for (a, m) in split_tiles(lo, hi, 512):
                            res.append((bo + a, m, s0 + a))
                    return res

                fams = [('u', HC, AF.Identity), ('zp', HC, AF.Silu), ('q', HC, AF.Identity), ('k', HC, AF.Identity),
                        ('v', HC, AF.Identity), ('za', HC, AF.Silu), ('gp', KC, AF.Sigmoid), ('ga', KC, AF.Sigmoid)]
                cb = 0
                for (fam, nblk, func) in fams:
                    tl = tiles_for(fam)
                    for j in range(nblk):
                        if tl:
                            wt, wr = wload(w_in_t[l, cb], KC * 128)
                            for (bs, n, s0) in tl:
                                ps, pr = nextps()
                                hres = [('hT', t) for t in range(bs // 128, (bs + n + 127) // 128)]
                                for kc in range(KC):
                                    mm(ps[:, 0:n], wt[:, kc * 128:(kc + 1) * 128], hT[:, kc, bs:bs + n], kc == 0,
                                       kc == KC - 1, wr + hres, [pr])
                                si = cnt['stg'] % 4
                                cnt['stg'] += 1
                                if fam == 'q':
                                    act(stg[si][:, 0:n], ps[:, 0:n], func, [pr, 'bqs'], [('stg', si)],
                                        bias=bqs[:, j:j + 1], scale=float(128 ** -0.5))
                                else:
                                    act(stg[si][:, 0:n], ps[:, 0:n], func, [pr, 'bT'], [('stg', si)],
                                        bias=bT[:, cb:cb + 1])
                                dma('sp', PT[l][cb * 128:(cb + 1) * 128, s0:s0 + n], stg[si][:, 0:n], [('stg', si)],
                                    [('PT', l, cb, s0)])
                            ada_pop()
                        cb += 1
                assert cb == NCB

            P.barrier()
            qoff = 2 * DP
            koff = 2 * DP + DP
            voff = 2 * DP + 2 * DP
            if l == 0:
                groups = [(4, 8), (12, 8), (20, 7)]
            else:
                groups = [(8, 8), (16, 8)]
            accs = [(psf[4], ('ps', 4), psf[5], ('ps', 5)), (psf[6], ('ps', 6), psf[7], ('ps', 7))]
            steps = []
            ucnt = [0]
            gcnt = [0]

            def head_loads(h):
                hb = h % 2
                dma('sp', KTh[hb][:], PT[l][koff + h * 128:koff + (h + 1) * 128, :], [], [('KTh', hb)])
                dma('sp', VTh[hb][:], PT[l][voff + h * 128:voff + (h + 1) * 128, :], [], [('VTh', hb)])
                dma('sp', QTh[hb][:], PT[l][qoff + h * 128:qoff + (h + 1) * 128, :], [], [('QTh', hb)])
                dma('sp', tregs[hb][:].rearrange("p a b -> p (a b)"), treg[l, h], [], [('treg', hb)])
                dma('sp', tspecs[hb][:].rearrange("p a b c -> p (a b c)"), tspec[l, h], [], [('tspec', hb)])

            def head_pre(h):
                hb = h % 2
                if h + 1 < NH:
                    head_loads(h + 1)
                cp(tregb[hb][:].rearrange("p a b -> p (a b)"), tregs[hb][:].rearrange("p a b -> p (a b)"),
                   [('treg', hb)], [('tregb', hb)])
                cp(tspecb[hb][:].rearrange("p a b c -> p (a b c)"), tspecs[hb][:].rearrange("p a b c -> p (a b c)"),
                   [('tspec', hb)], [('tspecb', hb)])
                for t8 in range(0, 17, 4):
                    nt = min(4, 17 - t8)
                    ps, pr = nextps()
                    psv = ps.ap().bitcast(BF16)
                    for k in range(nt):
                        st_ = t8 + k
                        P.add('pe', lambda e, st_=st_, k=k, psv=psv, hb=hb: e.transpose(
                            out=psv[:, k * 128:(k + 1) * 128], in_=VTh[hb][:, st_ * 128:(st_ + 1) * 128],
                            identity=identb[:]), reads=[('VTh', hb), 'identb'], writes=[pr], same=False)
                    cp(Vh[hb][:, t8:t8 + nt, :].rearrange("p a b -> p (a b)"), psv[:, 0:nt * 128], [pr], [('Vh', hb)])

            def add_unit(h, steplist, post):
                acc = accs[ucnt[0] % 2]
                ucnt[0] += 1
                for i_, sd in enumerate(steplist):
                    sd['h'] = h
                    sd['hb'] = h % 2
                    sd['acc'] = acc
                    sd['first'] = i_ == 0
                    sd['last'] = i_ == len(steplist) - 1
                    sd['post'] = post if sd['last'] else None
                    steps.append(sd)

            for h in range(NH):
                hb = h % 2
                for (r0, nr) in groups:
                    N = nr * 64
                    oi = gcnt[0] % 2
                    gcnt[0] += 1
                    sl = []
                    for c in range(2):
                        sl.append(dict(ks=NLAT + c * 128, vt=15 + c, qc0=r0 * 64, n=N, oc0=0, bias=None))
                    alo = ((r0 - 4) // 2) * 2
                    for a_ in range(alo, r0 + nr + 3, 2):
                        rlo = max(a_ - 3, r0)
                        rhi = min(a_ + 5, r0 + nr - 1)
                        if rlo > rhi:
                            continue
                        i0 = rlo - a_ + 3
                        cntr = rhi - rlo + 1
                        assert 0 <= a_ and a_ + 1 < 30
                        sl.append(dict(ks=a_ * 64, vt=a_ // 2, qc0=rlo * 64, n=cntr * 64, oc0=(rlo - r0) * 64,
                                       bias=('reg', i0, cntr)))
                    srows = [(si_, r) for si_, r in enumerate(SPEC_ROWS) if r0 <= r < r0 + nr]

                    def out_dma(h=h, r0=r0, N=N, oi=oi):
                        dma('sp', AOT[l][h * 128:(h + 1) * 128, r0 * 64:r0 * 64 + N], ostg[oi][:, 0:N], [('ostg', oi)],
                            [('AOT', l, h, r0)])

                    def post_reg(acc, N=N, oi=oi, has_spec=bool(srows), out_dma=out_dma):
                        psO, rO, psS_, rS_ = acc
                        ri = cnt['stg'] % 2
                        cnt['stg'] += 1
                        P.add('dve', lambda e: e.reciprocal(out=rsb[ri][:, 0:N], in_=psS_[:, 0:N]), reads=[rS_],
                              writes=[('rsb', ri)])
                        tt(ostg[oi][:, 0:N], psO[:, 0:N], rsb[ri][:, 0:N], ALU.mult, [rO, ('rsb', ri)], [('ostg', oi)])
                        if not has_spec:
                            out_dma()

                    add_unit(h, sl, post_reg)
                    if srows:
                        sl = []
                        for k_, (si_, r) in enumerate(srows):
                            for c in range(2):
                                sl.append(dict(ks=NLAT + c * 128, vt=15 + c, qc0=r * 64, n=64, oc0=k_ * 64, bias=None))
                            abase = 4 if r <= 11 else 16
                            for ci in range(6):
                                a_ = abase + 2 * ci
                                sl.append(dict(ks=a_ * 64, vt=a_ // 2, qc0=r * 64, n=64, oc0=k_ * 64, bias=('spec', si_, ci)))
                        j0 = (srows[0][1] - r0) * 64
                        NS = len(srows) * 64

                        def post_spec(acc, NS=NS, j0=j0, oi=oi, out_dma=out_dma):
                            psO, rO, psS_, rS_ = acc
                            ri = cnt['stg'] % 2
                            cnt['stg'] += 1
                            P.add('dve', lambda e: e.reciprocal(out=rsb[ri][:, 0:NS], in_=psS_[:, 0:NS]), reads=[rS_],
                                  writes=[('rsb', ri)])
                            tt(ostg[oi][:, j0:j0 + NS], psO[:, 0:NS], rsb[ri][:, 0:NS], ALU.mult, [rO, ('rsb', ri)],
                               [('ostg', oi)])
                            out_dma()

                        add_unit(h, sl, post_spec)
                if l == 0:
                    oi = gcnt[0] % 2
                    gcnt[0] += 1
                    sl = [dict(ks=NLAT + c * 128, vt=15 + c, qc0=NLAT, n=256, oc0=0, bias=None) for c in range(2)]

                    def post_ctx(acc, oi=oi, h=h):
                        psO, rO, psS_, rS_ = acc
                        ri = cnt['stg'] % 2
                        cnt['stg'] += 1
                        P.add('dve', lambda e: e.reciprocal(out=rsb[ri][:, 0:256], in_=psS_[:, 0:256]), reads=[rS_],
                              writes=[('rsb', ri)])
                        tt(ostg[oi][:, 0:256], psO[:, 0:256], rsb[ri][:, 0:256], ALU.mult, [rO, ('rsb', ri)], [('ostg', oi)])
                        dma('sp', AOT[l][h * 128:(h + 1) * 128, NLAT:NLAT + 256], ostg[oi][:, 0:256], [('ostg', oi)],
                            [('AOT', l, h, 'c')])

                    add_unit(h, sl, post_ctx)

            LA = 3

            def front(i):
                sd = steps[i]
                h, hb = sd['h'], sd['hb']
                if i == 0 or steps[i - 1]['h'] != h:
                    head_pre(h)
                n = sd['n']
                ps, pr = nextps()
                pi_ = i % 4
                sd['pi'] = pi_
                if sd['bias'] is not None:
                    if sd['bias'][0] == 'reg':
                        _, i0, cntr = sd['bias']
                        bias_ap = tregb[hb][:, i0:i0 + cntr, :].rearrange("p a b -> p (a b)")
                        bres = ('tregb', hb)
                    else:
                        _, si_, ci = sd['bias']
                        bias_ap = tspecb[hb][:, si_, ci, :]
                        bres = ('tspecb', hb)
                    mm(ps[:, 0:n], KTh[hb][:, sd['ks']:sd['ks'] + 128], QTh[hb][:, sd['qc0']:sd['qc0'] + n], True, False,
                       [('KTh', hb), ('QTh', hb)], [pr])
                    mm(ps[:, 0:n], identb[:], bias_ap, False, True, ['identb', bres], [pr])
                else:
                    mm(ps[:, 0:n], KTh[hb][:, sd['ks']:sd['ks'] + 128], QTh[hb][:, sd['qc0']:sd['qc0'] + n], True, True,
                       [('KTh', hb), ('QTh', hb)], [pr])
                act(pTb[pi_][:, 0:n], ps[:, 0:n], AF.Exp, [pr], [('pTb', pi_)])

            def back(i):
                sd = steps[i]
                hb = sd['hb']
                psO, rO, psS_, rS_ = sd['acc']
                n, oc0, pi_ = sd['n'], sd['oc0'], sd['pi']
                mm(psO[:, oc0:oc0 + n], Vh[hb][:, sd['vt'], :], pTb[pi_][:, 0:n], sd['first'], sd['last'],
                   [('Vh', hb), ('pTb', pi_)], [rO], skip=True)
                mm(psS_[:, oc0:oc0 + n], onesb[:], pTb[pi_][:, 0:n], sd['first'], sd['last'], ['onesb', ('pTb', pi_)], [rS_],
                   skip=True)
                if sd['post'] is not None:
                    sd['post'](sd['acc'])

            head_loads(0)
            ns_ = len(steps)
            for i in range(ns_ + LA):
                if i < ns_:
                    front(i)
                if i - LA >= 0:
                    back(i - LA)

            P.barrier(engines=('pe', 'act', 'dve', 'sp', 'pool'))
            if l == 0:
                supertiles = [[(256, 512, 0), (768, 512, 0)], [(1280, 448, 0), (1920, 256, 1)]]
            else:
                supertiles = [[(512, 512, 0), (1024, 512, 0)]]
            for g in range(4):
                P.add('pool', lambda e, l=l, g=g: e.dma_start(out=wpool[:, g].rearrange("p c d -> p (c d)"),
                                                           in_=w_pool_t[l, g], max_dma_last_dim=4096),
                      writes=[('wpool', g)], dma=True)
            zoff = DP
            zaoff = 2 * DP + 3 * DP
            gpoff = 6 * DP
            gaoff = 6 * DP + D
            for sti, stl in enumerate(supertiles):
                if sti > 0:
                    P.barrier()
                for si, (s0, T, isctx) in enumerate(stl):
                    co = si * 512
                    for g in range(4):
                        w = POOL_WINDOWS[g]
                        pg = g % 2
                        for cc in range(GC):
                            fc = g * GC + cc
                            ui = cnt['x'] % 2
                            cnt['x'] += 1
                            if isctx:
                                P.add('dve', lambda e, ui=ui, T=T: e.memset(ub[ui][:, 0:T + 16], 0.0), writes=[('ub', ui)])
                                dma('sp', ub[ui][:, 8:8 + T], PT[l][fc * 128:(fc + 1) * 128, s0:s0 + T], [], [('ub', ui)])
                                cp(um[ui][:, 0:T + 16], ub[ui][:, 0:T + 16], [('ub', ui)], [('um', ui)])
                            else:
                                dma('sp', ub[ui][:, 0:T + 16], PT[l][fc * 128:(fc + 1) * 128, s0 - 8:s0 + T + 8], [],
                                    [('ub', ui)])
                                tt(um[ui][:, 0:T + 16], ub[ui][:, 0:T + 16], vm[:, s0:s0 + T + 16], ALU.mult,
                                   [('ub', ui), 'vm'], [('um', ui)], eng=('pool' if cc % 2 == 1 else 'dve'))
                            srcb, rsrc = um[ui], ('um', ui)
                            ln = T + 16
                            step = 1
                            peng = 'pool' if cc % 2 == 1 else 'dve'
                            bufs = [(pa2, 'pa2'), (pb2, 'pb2')] if cc % 2 == 1 else [(pa, 'pa'), (pb, 'pb')]
                            bi = 0
                            while step < w:
                                dst, rdst = bufs[bi]
                                bi ^= 1
                                ln2 = ln - step
                                tt(dst[:, 0:ln2], srcb[:, 0:ln2], srcb[:, step:step + ln2], ALU.add, [rsrc], [rdst], eng=peng)
                                srcb, rsrc = dst, rdst
                                ln = ln2
                                step *= 2
                            o0 = 8 - w // 2
                            stt(pTg[pg][:, cc, 0:T], srcb[:, o0:o0 + T], 1.0 / w, um[ui][:, 8:8 + T], ALU.mult,
                                ALU.subtract, [rsrc, ('um', ui)], [('pTg', pg)])
                            tab = invc if isctx else invl
                            tabr = 'invc' if isctx else 'invl'
                            if isctx:
                                fixes = [(0, 0), (248, 8)]
                            else:
                                fixes = [(fs - s0, to) for (fs, to) in ((512, 0), (1528, 8))
                                         if s0 <= fs and fs + 8 <= s0 + T]
                            for (c0, to) in fixes:
                                dst, rdst = bufs[bi]
                                tt(dst[:, 0:8], srcb[:, o0 + c0:o0 + c0 + 8], tab[:, g * 16 + to:g * 16 + to + 8], ALU.mult,
                                   [rsrc, tabr], [rdst])
                                tt(pTg[pg][:, cc, c0:c0 + 8], dst[:, 0:8], um[ui][:, 8 + c0:8 + c0 + 8], ALU.subtract,
                                   [rdst, ('um', ui)], [('pTg', pg)])
                        for db in range(GC):
                            fcd = g * GC + db
                            ps, pr = nextps()
                            for cc in range(GC):
                                mm(ps[:, 0:T], wpool[:, g, cc, db * 128:(db + 1) * 128], pTg[pg][:, cc, 0:T], cc == 0,
                                   cc == GC - 1, [('wpool', g), ('pTg', pg)], [pr])
                            zi = cnt['stg'] % 3
                            cnt['stg'] += 1
                            dma('sp', zt[zi][:, 0:T], PT[l][zoff + fcd * 128:zoff + (fcd + 1) * 128, s0:s0 + T], [],
                                [('zt', zi)])
                            stt(YP[:, fcd, co:co + T], ps[:, 0:T], spTs[:, fcd:fcd + 1], zt[zi][:, 0:T], ALU.mult, ALU.mult,
                                [pr, 'spTs', ('zt', zi)], [('YP', fcd, si)])
                    for fc in range(HC):
                        zi = cnt['stg'] % 3
                        cnt['stg'] += 1
                        dma('sp', zt[zi][:, 0:T], PT[l][zaoff + fc * 128:zaoff + (fc + 1) * 128, s0:s0 + T], [],
                            [('zt', zi)])
                        dma('sp', at_[zi][:, 0:T], AOT[l][fc * 128:(fc + 1) * 128, s0:s0 + T], [], [('at', zi)])
                        tt(YA[:, fc, co:co + T], at_[zi][:, 0:T], zt[zi][:, 0:T], ALU.mult, [('at', zi), ('zt', zi)],
                           [('YA', fc, si)], eng='pool')
                for dc in range(KC):
                    wp, wpr = wload(w_brp_t[l, dc], HC * 128)
                    wa, war = wload(w_bra_t[l, dc], HC * 128)
                    for si, (s0, T, isctx) in enumerate(stl):
                        co = si * 512
                        ps1, pr1 = nextps()
                        for c in range(HC):
                            mm(ps1[:, 0:T], wp[:, c * 128:(c + 1) * 128], YP[:, c, co:co + T], c == 0, c == HC - 1,
                               wpr + [('YP', c, si)], [pr1])
                        ps2, pr2 = nextps()
                        for c in range(HC):
                            mm(ps2[:, 0:T], wa[:, c * 128:(c + 1) * 128], YA[:, c, co:co + T], c == 0, c == HC - 1,
                               war + [('YA', c, si)], [pr2])
                        gi = cnt['x'] % 2
                        cnt['x'] += 1
                        dma('sp', gpt[gi][:, 0:T], PT[l][gpoff + dc * 128:gpoff + (dc + 1) * 128, s0:s0 + T], [],
                            [('gpt', gi)])
                        dma('sp', gat[gi][:, 0:T], PT[l][gaoff + dc * 128:gaoff + (dc + 1) * 128, s0:s0 + T], [],
                            [('gat', gi)])
                        tt(t1[gi][:, 0:T], ps1[:, 0:T], gpt[gi][:, 0:T], ALU.mult, [pr1, ('gpt', gi)], [('t1', gi)])
                        tt(t2[gi][:, 0:T], ps2[:, 0:T], gat[gi][:, 0:T], ALU.mult, [pr2, ('gat', gi)], [('t2', gi)])
                        tt(mts[gi][:, 0:T], t1[gi][:, 0:T], t2[gi][:, 0:T], ALU.add, [('t1', gi), ('t2', gi)],
                           [('mts', gi)])
                        dma('act', MTD[l][dc * 128:(dc + 1) * 128, s0:s0 + T], mts[gi][:, 0:T], [('mts', gi)],
                            [('MTD', l, dc, s0)])
                    ada_pop()
                P.barrier()
                for si, (s0, T, isctx) in enumerate(stl):
                    co = si * 512
                    dma('sp', MT[:, :, co:co + T], MTD[l][:, s0:s0 + T].rearrange("(kc p) s -> p kc s", p=128), [],
                        [('MT', si)])
                def epilogue(fg, stl=stl):
                    for si, (s0, T, isctx) in enumerate(stl):
                        if isctx:
                            xs_ap, xd_ap, xrow0 = ctxb, c1, 0
                        else:
                            xs_ap = xw if l == 0 else x1
                            xd_ap = x1 if l == 0 else x2
                            xrow0 = s0
                        for jt in range((T + 127) // 128):
                            tn = min(128, T - jt * 128)
                            pi = 6 + (cnt['bt'] % 2)
                            cnt['bt'] += 1
                            for q4 in range(4):
                                P.add('pe', lambda e, q4=q4, jt=jt, pi=pi, tn=tn, si=si: e.transpose(
                                    out=psf[pi][0:tn, q4 * 128:(q4 + 1) * 128], in_=tT[si][q4][:, jt * 128:jt * 128 + tn],
                                    identity=identf[:]), reads=[('tT', si, q4), 'identf'], writes=[('ps', pi)], same=False)
                            xi = cnt['x'] % 3
                            cnt['x'] += 1
                            row = xrow0 + jt * 128
                            dma('sp', xr[xi][0:tn, :], xs_ap[row:row + tn, fg * 512:(fg + 1) * 512], [], [('xr', xi)])
                            tt(xn[xi][0:tn, :], psf[pi][0:tn, :], xr[xi][0:tn, :], ALU.add, [('ps', pi), ('xr', xi)],
                               [('xn', xi)])
                            dma('act', xd_ap[row:row + tn, fg * 512:(fg + 1) * 512], xn[xi][0:tn, :], [('xn', xi)],
                                [('xd', l, isctx, row, fg)])

                pend_epi = None
                for fg in range(KC // 4):
                    for q4 in range(4):
                        dc2 = fg * 4 + q4
                        wo, wor = wload(w_out_t[l, dc2], KC * 128)
                        pss = []
                        for si, (s0, T, isctx) in enumerate(stl):
                            co = si * 512
                            ps, pr = nextps()
                            for kc in range(KC):
                                mm(ps[:, 0:T], wo[:, kc * 128:(kc + 1) * 128], MT[:, kc, co:co + T], kc == 0, kc == KC - 1,
                                   wor + [('MT', si)], [pr])
                            pss.append((ps, pr))
                        if q4 == 0 and pend_epi is not None:
                            epilogue(pend_epi)
                            pend_epi = None
                        for si, (s0, T, isctx) in enumerate(stl):
                            ps, pr = pss[si]
                            tsc(tT[si][q4][:, 0:T], ps[:, 0:T], Gv(isctx, dc2), None, ALU.mult, None,
                                [pr, ('adag', lp, isctx)], [('tT', si, q4)])
                        ada_pop()
                    pend_epi = fg
                epilogue(pend_epi)

        P.barrier()
        dma('sp', fgs[:], fg_rep, [], ['fgs'])
        for jt in range(8):
            xi = jt % 2
            sc0 = 4 * xi
            xh = xhb[xi]
            row = 512 + jt * 128
            dma('sp', xt[xi][:], x2[row:row + 128, :], [], [('xt', xi)])
            act(xh[:], xt[xi][:], AF.Square, [('xt', xi)], [('xh', xi), ('ss0', xi)], accum_out=ss[:, sc0:sc0 + 1])
            tsc(ss[:, sc0 + 1:sc0 + 2], ss[:, sc0:sc0 + 1], 1.0 / D, 1e-6, ALU.mult, ALU.add, [('ss0', xi)], [('ss1', xi)])
            act(ss[:, sc0 + 2:sc0 + 3], ss[:, sc0 + 1:sc0 + 2], AF.Sqrt, [('ss1', xi)], [('ss2', xi)])
            P.add('dve', lambda e, sc0=sc0: e.reciprocal(out=ss[:, sc0 + 3:sc0 + 4], in_=ss[:, sc0 + 2:sc0 + 3]),
                  reads=[('ss2', xi)], writes=[('ss3', xi)])
            stt(xt[xi][:], xt[xi][:], ss[:, sc0 + 3:sc0 + 4], fgs[:], ALU.mult, ALU.mult, [('xt', xi), ('ss3', xi), 'fgs'],
                [('xt', xi)])
            dma('act', y[jt * 128:(jt + 1) * 128, :], xt[xi][:], [('xt', xi)], [('y', jt)])
        P.emit()
        build.stats = dict(nops={e: len(P.ops[e]) for e in ENGS}, maxcnt=P.maxcnt)
    return nc


def _tile_w(w, kc, ncb):
    K, C = w.shape
    return np.ascontiguousarray(w.reshape(kc, 128, ncb, 128).transpose(2, 1, 0, 3)).reshape(ncb, 128, kc * 128)


def _fm(v):
    return np.ascontiguousarray(v.reshape(-1, 128).T)


def _base_tables(rpb_l, NH):
    kc = np.arange(64)[:, None]
    qc = np.arange(64)[None, :]
    ws = np.clip(qc - 8, 0, 48)
    colvalid = (kc >= ws) & (kc < ws + 16)
    dc = np.clip(kc - qc + 15, 0, 30)
    base = np.where(colvalid[None, None], rpb_l[:, :, dc], np.float32(NEG)).astype(np.float32)
    return base


def _host_prep(inputs, D):
    L = inputs["w_in"].shape[0]
    KC = D // 128
    HC = KC // 2
    NH = HC
    GP = D // 8
    GC = GP // 128
    f32 = np.float32
    shared = {}
    shared["ident"] = np.eye(128, dtype=f32)
    shared["w_in_t"] = np.stack([_tile_w(np.asarray(inputs["w_in"][l], f32), KC, 5 * KC) for l in range(L)])
    shared["w_ada_t"] = np.stack([_tile_w(np.asarray(inputs["w_ada"][l], f32), KC, 3 * KC) for l in range(L)])
    shared["w_out_t"] = np.stack([_tile_w(np.asarray(inputs["w_out"][l], f32), KC, KC) for l in range(L)])
    shared["w_brp_t"] = np.stack([_tile_w(np.asarray(inputs["w_br_pool"][l], f32), HC, KC) for l in range(L)])
    shared["w_bra_t"] = np.stack([_tile_w(np.asarray(inputs["w_br_attn"][l], f32), HC, KC) for l in range(L)])
    wp = np.asarray(inputs["w_pool"], f32)
    shared["w_pool_t"] = np.ascontiguousarray(wp.reshape(L, 4, GC, 128, GP).transpose(0, 1, 3, 2, 4)).reshape(L, 4, 128, GC * GP)
    shared["b_in_T"] = np.stack([_fm(np.asarray(inputs["b_in"][l], f32)) for l in range(L)])
    shared["b_ada_T"] = np.stack([_fm(np.asarray(inputs["b_ada"][l], f32)) for l in range(L)])
    shared["g_T"] = np.stack([_fm(np.asarray(inputs["norm_g"][l], f32)) for l in range(L)])
    shared["sp_T"] = np.stack([_fm(np.asarray(inputs["s_pool"][l], f32)) for l in range(L)])
    shared["fg_rep"] = np.ascontiguousarray(np.broadcast_to(np.asarray(inputs["final_g"], f32)[None, :], (128, D)))
    rpb = np.asarray(inputs["rpb"], f32)
    bases = [_base_tables(rpb[l], NH) for l in range(L)]
    negblk = np.full((NH, 64, 64), NEG, f32)

    def base_dr(l, dr):
        if -7 <= dr <= 7:
            return bases[l][:, dr + 7]
        return negblk

    treg = np.empty((L, NH, 128, 9, 64), f32)
    for l in range(L):
        for i in range(9):
            dlo = 3 - i
            dhi = 4 - i
            treg[l, :, 0:64, i, :] = base_dr(l, dlo) if -4 <= dlo <= 3 else negblk
            treg[l, :, 64:128, i, :] = base_dr(l, dhi) if -4 <= dhi <= 3 else negblk
    shared["treg"] = treg.reshape(L, NH, 128, 9 * 64)
    invc = np.zeros((128, 64), f32)
    for g, w in enumerate(POOL_WINDOWS):
        for k, t in enumerate(list(range(0, 8)) + list(range(248, 256))):
            lo = np.clip(t - w // 2, 0, 255)
            hi = np.clip(t - w // 2 + w - 1, 0, 255)
            invc[:, g * 16 + k] = 1.0 / (hi - lo + 1)
    shared["invc"] = invc

    x = np.asarray(inputs["x"], f32)
    ctx = np.asarray(inputs["ctx"], f32)
    c = np.asarray(inputs["c"], f32)
    c_ctx = np.asarray(inputs["c_ctx"], f32)
    maps = []
    for core in range(8):
        b, j = core // 4, core % 4
        R0 = 16 * j - 8
        m = dict(shared)
        xwin = np.zeros((30, 64, D), f32)
        rows = np.arange(R0, R0 + 30)
        ok = (rows >= 0) & (rows < 64)
        xg = x[b].reshape(64, 64, D)
        xwin[ok] = xg[rows[ok]]
        m["xw"] = xwin.reshape(NLAT, D)
        m["ctxb"] = np.ascontiguousarray(ctx[b])
        cT = np.empty((128, KC, 2), f32)
        cT[:, :, 0] = _fm(c[b])
        cT[:, :, 1] = _fm(c_ctx)
        m["cT"] = cT.reshape(128, KC * 2)
        vm = np.zeros((128, NLAT + 16), f32)
        tokvalid = np.repeat(ok, 64).astype(f32)
        vm[:, 8:8 + NLAT] = tokvalid[None, :]
        m["vm"] = vm
        invl = np.zeros((128, 64), f32)
        for g, w in enumerate(POOL_WINDOWS):
            for k in range(16):
                slot = (512 + k) if k < 8 else (1528 + k - 8)
                t = R0 * 64 + slot
                if 0 <= t < 4096:
                    lo = np.clip(t - w // 2, 0, 4095)
                    hi = np.clip(t - w // 2 + w - 1, 0, 4095)
                    invl[:, g * 16 + k] = 1.0 / (hi - lo + 1)
                else:
                    invl[:, g * 16 + k] = 1.0 / w
        m["invl"] = invl
        ts = np.empty((L, NH, 128, 7, 6, 64), f32)
        for si, r in enumerate(SPEC_ROWS):
            gr = R0 + r
            abase = 4 if r <= 11 else 16
            if 0 <= gr < 64:
                rs = int(np.clip(gr - 4, 0, 56))
                vrows = set(range(rs, rs + 8))
            else:
                vrows = None
            for ci in range(6):
                for half in range(2):
                    kr = abase + 2 * ci + half
                    gk = R0 + kr
                    dr = kr - r
                    if vrows is None:
                        valid = -4 <= dr <= 3
                    else:
                        valid = gk in vrows
                    for l in range(L):
                        ts[l, :, half * 64:(half + 1) * 64, si, ci, :] = base_dr(l, dr) if valid else negblk
        m["tspec"] = ts.reshape(L, NH, 128, 7 * 6 * 64)
        maps.append(m)
    return maps


_CACHE = {}


def kernel(**inputs):
    D = int(inputs["x"].shape[-1])
    if D not in _CACHE:
        _CACHE[D] = build(D, L=int(inputs["w_in"].shape[0]))
    nc = _CACHE[D]
    maps = _host_prep(inputs, D)
    res = run_bass_kernel_spmd(nc, maps, core_ids=list(range(8)))
    out = np.empty((2, 4096, D), np.float32)
    for core in range(8):
        b, j = core // 4, core % 4
        out[b, j * 1024:(j + 1) * 1024, :] = np.asarray(res.results[core]["y"], np.float32)
    return out
```
